# Optimizing a Trainium2 kernel written in Bass

```python
import math
import jax, jax.numpy as jnp
from jax import lax
import numpy as np

D_MODEL = 4096
BATCH = 4
SEQ = 2048
DEPTH = 2
DEC_BATCH = 32
DEC_SEQ = 4
PAST_LEN = 16384
PAGE_SIZE = 128

N_EVEN = (DEPTH + 1) // 2
N_ODD = DEPTH // 2

D_S5 = D_MODEL // 2
S5_GROUP = 16
S5_GROUPS = D_S5 // S5_GROUP
S5_STATE = 64

HEAD_DIM = 64
SWA_HEADS = (D_MODEL // 2) // HEAD_DIM
SWA_KV_HEADS = 8
SWA_GQ = SWA_HEADS // SWA_KV_HEADS
D_SWA_Q = SWA_HEADS * HEAD_DIM
D_SWA_KV = SWA_KV_HEADS * HEAD_DIM
WINDOW = 128
SWA_BLOCK = 128
N_BUCKETS = 32
MAX_DISTANCE = 128

HG_DK = 128
HG_HEADS = D_MODEL // HG_DK
HG_DV = D_MODEL // HG_HEADS
HG_CHUNK = 32

N_MEM = 256
MEM_HEADS = 4
MEM_HEAD_DIM = 128
D_MEM = MEM_HEADS * MEM_HEAD_DIM

D_FF = 256 * ((-(-8 * D_MODEL // 3) + 255) // 256)

ALPHA = (2 * DEPTH) ** 0.25
BETA = (8 * DEPTH) ** -0.25
LN_EPS = 1e-5
RMS_EPS = 1e-6
NEG_BIG = -1e30

kernel_name = "s5_swa_hgrn2_memory_hybrid_step"


def layer_norm(x, g, b):
    xf = x.astype(jnp.float32)
    mu = jnp.mean(xf, axis=-1, keepdims=True)
    var = jnp.mean(jnp.square(xf - mu), axis=-1, keepdims=True)
    return ((xf - mu) * lax.rsqrt(var + LN_EPS) * g.astype(jnp.float32) + b.astype(jnp.float32)).astype(x.dtype)


def rms_norm(x, g):
    xf = x.astype(jnp.float32)
    return xf * lax.rsqrt(jnp.mean(xf * xf, axis=-1, keepdims=True) + RMS_EPS) * g.astype(jnp.float32)


def t5_bucket(dist):
    n = jnp.maximum(dist, 0)
    max_exact = N_BUCKETS // 2
    nf = jnp.maximum(n, 1).astype(jnp.float32)
    large = max_exact + (jnp.log(nf / max_exact) / math.log(MAX_DISTANCE / max_exact)
                         * (N_BUCKETS - max_exact)).astype(jnp.int32)
    large = jnp.minimum(large, N_BUCKETS - 1)
    return jnp.where(n < max_exact, n, large)


def s5_discretize(lam_re, lam_im, log_dt, b_re, b_im):
    f32 = jnp.float32
    lr = jnp.minimum(lam_re.astype(f32), -1e-4)
    li = lam_im.astype(f32)
    dt = jnp.exp(log_dt.astype(f32))[:, None]
    mag = jnp.exp(lr * dt)
    a_re = mag * jnp.cos(li * dt)
    a_im = mag * jnp.sin(li * dt)
    den = lr * lr + li * li
    fr = ((a_re - 1.0) * lr + a_im * li) / den
    fi = (a_im * lr - (a_re - 1.0) * li) / den
    br, bi = b_re.astype(f32), b_im.astype(f32)
    bb_re = fr[..., None] * br - fi[..., None] * bi
    bb_im = fr[..., None] * bi + fi[..., None] * br
    return a_re, a_im, bb_re, bb_im


def _complex_affine_combine(e1, e2):
    a1r, a1i, b1r, b1i = e1
    a2r, a2i, b2r, b2i = e2
    return (a2r * a1r - a2i * a1i,
            a2r * a1i + a2i * a1r,
            a2r * b1r - a2i * b1i + b2r,
            a2r * b1i + a2i * b1r + b2i)


def s5_scan(u, h0_re, h0_im, a_re, a_im, bb_re, bb_im, c_re, c_im, d_skip):
    f32 = jnp.float32
    uf = u.astype(f32)
    br = jnp.einsum('gph,btgh->btgp', bb_re, uf)
    bi = jnp.einsum('gph,btgh->btgp', bb_im, uf)
    h0r, h0i = h0_re.astype(f32), h0_im.astype(f32)
    br = br.at[:, 0].add(a_re * h0r - a_im * h0i)
    bi = bi.at[:, 0].add(a_re * h0i + a_im * h0r)
    t = u.shape[1]
    ar = jnp.broadcast_to(a_re, (1, t) + a_re.shape)
    ai = jnp.broadcast_to(a_im, (1, t) + a_im.shape)
    _, _, hr, hi = lax.associative_scan(_complex_affine_combine, (ar, ai, br, bi), axis=1)
    y = (jnp.einsum('ghp,btgp->btgh', c_re.astype(f32), hr)
         - jnp.einsum('ghp,btgp->btgh', c_im.astype(f32), hi)
         + d_skip.astype(f32) * uf)
    return y, hr[:, -1], hi[:, -1]


def swa_attend(q, k, v, q_pos, k_pos, sinks, rel_bias):
    f32 = jnp.float32
    lead = q.shape[:-3]
    tq, tk = q.shape[-3], k.shape[-3]
    qg = q.reshape(*lead, tq, SWA_KV_HEADS, SWA_GQ, HEAD_DIM)
    s = jnp.einsum('...qhgd,...shd->...hgqs', qg, k).astype(f32) * (HEAD_DIM ** -0.5)
    dist = q_pos[..., :, None] - k_pos[..., None, :]
    valid = (dist >= 0) & (dist < WINDOW) & (k_pos[..., None, :] >= 0)
    bias = jnp.moveaxis(rel_bias[t5_bucket(dist)], -1, -3)
    bias = bias.reshape(*bias.shape[:-3], SWA_KV_HEADS, SWA_GQ, tq, tk)
    s = s + bias.astype(f32)
    s = jnp.where(valid[..., None, None, :, :], s, NEG_BIG)
    sk = sinks.astype(f32).reshape(SWA_KV_HEADS, SWA_GQ)[..., None, None]
    m = jnp.maximum(jnp.max(s, axis=-1, keepdims=True), sk)
    p = jnp.exp(s - m)
    p = p / (jnp.sum(p, axis=-1, keepdims=True) + jnp.exp(sk - m))
    o = jnp.einsum('...hgqs,...shd->...qhgd', p.astype(v.dtype), v)
    return o.reshape(*lead, tq, SWA_HEADS * HEAD_DIM)


def swa_prompt(q, k, v, sinks, rel_bias):
    b, t = q.shape[:2]
    nb = t // SWA_BLOCK
    qb = q.reshape(b, nb, SWA_BLOCK, SWA_HEADS, HEAD_DIM)
    kb = k.reshape(b, nb, SWA_BLOCK, SWA_KV_HEADS, HEAD_DIM)
    vb = v.reshape(b, nb, SWA_BLOCK, SWA_KV_HEADS, HEAD_DIM)
    kk = jnp.concatenate([jnp.concatenate([jnp.zeros_like(kb[:, :1]), kb[:, :-1]], axis=1), kb], axis=2)
    vv = jnp.concatenate([jnp.concatenate([jnp.zeros_like(vb[:, :1]), vb[:, :-1]], axis=1), vb], axis=2)
    pos = jnp.arange(t, dtype=jnp.int32).reshape(nb, SWA_BLOCK)
    k_pos = jnp.concatenate([pos - SWA_BLOCK, pos], axis=1)
    o = swa_attend(qb, kk, vv, pos, k_pos, sinks, rel_bias)
    return o.reshape(b, t, D_SWA_Q)


def swa_sample(q, k, v, buf_k, buf_v, sinks, rel_bias):
    t = q.shape[1]
    w_buf = buf_k.shape[1]
    kk = jnp.concatenate([buf_k.astype(k.dtype), k], axis=1)
    vv = jnp.concatenate([buf_v.astype(v.dtype), v], axis=1)
    q_pos = PAST_LEN + jnp.arange(t, dtype=jnp.int32)
    k_pos = jnp.concatenate([PAST_LEN - w_buf + jnp.arange(w_buf, dtype=jnp.int32), q_pos])
    o = swa_attend(q, kk, vv, q_pos, k_pos, sinks, rel_bias)
    return o, kk[:, -w_buf:], vv[:, -w_buf:]


def even_mixer(x, h0_re, h0_im, buf_k, buf_v, w_in, s5p, sinks, rel_bias, w_out, w_buf):
    b, t, _ = x.shape
    a_re, a_im, bb_re, bb_im, c_re, c_im, d_skip, w_glu = s5p
    h = x @ w_in
    u, q, k, v = jnp.split(h, [D_S5, D_S5 + D_SWA_Q, D_S5 + D_SWA_Q + D_SWA_KV], axis=-1)
    y, hr, hi = s5_scan(u.reshape(b, t, S5_GROUPS, S5_GROUP), h0_re, h0_im,
                        a_re, a_im, bb_re, bb_im, c_re, c_im, d_skip)
    z = jax.nn.gelu(y.reshape(b, t, D_S5)).astype(x.dtype)
    s5_out = z * jax.nn.sigmoid(z @ w_glu)
    q = q.reshape(b, t, SWA_HEADS, HEAD_DIM)
    k = k.reshape(b, t, SWA_KV_HEADS, HEAD_DIM)
    v = v.reshape(b, t, SWA_KV_HEADS, HEAD_DIM)
    if buf_k is None:
        att = swa_prompt(q, k, v, sinks, rel_bias)
        new_k, new_v = k[:, -w_buf:], v[:, -w_buf:]
    else:
        att, new_k, new_v = swa_sample(q, k, v, buf_k, buf_v, sinks, rel_bias)
    out = jnp.concatenate([s5_out, att.astype(x.dtype)], axis=-1) @ w_out
    return out, hr, hi, new_k, new_v


def hgrn2_recurrence(q, k, v, logf, s0):
    b, t = q.shape[:2]
    pad = (-t) % HG_CHUNK
    if pad:
        pw = ((0, 0), (0, pad), (0, 0), (0, 0))
        q, k, v, logf = (jnp.pad(a, pw) for a in (q, k, v, logf))
    nc = (t + pad) // HG_CHUNK

    def chunks(a):
        return a.reshape(b, nc, HG_CHUNK, HG_HEADS, a.shape[-1]).transpose(1, 0, 3, 2, 4)

    q, k, v, logf = chunks(q), chunks(k), chunks(v), chunks(logf)
    cum = jnp.cumsum(logf, axis=-2)
    q_t = q * jnp.exp(cum)
    k_t = k * jnp.exp(-cum)
    k_last = k * jnp.exp(cum[..., -1:, :] - cum)
    decay_last = jnp.exp(cum[..., -1, :])
    mask = jnp.tril(jnp.ones((HG_CHUNK, HG_CHUNK), dtype=bool))
    attn = jnp.where(mask, jnp.einsum('nbhcd,nbhsd->nbhcs', q_t, k_t), 0.0)
    o_intra = jnp.einsum('nbhcs,nbhsv->nbhcv', attn, v)

    def step(s, inp):
        qc, kc, vc, dc = inp
        o_inter = jnp.einsum('bhcd,bhdv->bhcv', qc, s)
        s = dc[..., None] * s + jnp.einsum('bhcd,bhcv->bhdv', kc, vc)
        return s, o_inter

    s_fin, o_inter = lax.scan(step, s0, (q_t, k_last, v, decay_last))
    o = (o_intra + o_inter).transpose(1, 0, 3, 2, 4).reshape(b, nc * HG_CHUNK, HG_HEADS, HG_DV)[:, :t]
    return o, s_fin


def odd_mixer(x, s0, lb, w_in, norm_g, w_out):
    b, t, _ = x.shape
    f32 = jnp.float32
    q, fz, i, g = jnp.split(x @ w_in, 4, axis=-1)
    q = jax.nn.silu(q.astype(f32))
    f = lb + (1.0 - lb) * jax.nn.sigmoid(fz.astype(f32))
    logf = jnp.log(f)
    k = 1.0 - f
    o, s_new = hgrn2_recurrence(q.reshape(b, t, HG_HEADS, HG_DK), k.reshape(b, t, HG_HEADS, HG_DK),
                                i.astype(f32).reshape(b, t, HG_HEADS, HG_DV),
                                logf.reshape(b, t, HG_HEADS, HG_DK), s0.astype(f32))
    o = rms_norm(o.reshape(b, t, D_MODEL), norm_g) * jax.nn.sigmoid(g.astype(f32))
    return o.astype(x.dtype) @ w_out, s_new


def memory_kv(mem, wk, wv):
    b = mem.shape[0]
    return ((mem @ wk).reshape(b, N_MEM, MEM_HEADS, MEM_HEAD_DIM),
            (mem @ wv).reshape(b, N_MEM, MEM_HEADS, MEM_HEAD_DIM))


def cross_attn(x, mk, mv, wq, wo):
    b, t, _ = x.shape
    q = (x @ wq).reshape(b, t, MEM_HEADS, MEM_HEAD_DIM)
    s = jnp.einsum('bthd,bshd->bhts', q, mk.astype(q.dtype)).astype(jnp.float32) * (MEM_HEAD_DIM ** -0.5)
    p = jax.nn.softmax(s, axis=-1)
    o = jnp.einsum('bhts,bshd->bthd', p.astype(x.dtype), mv.astype(x.dtype)).reshape(b, t, D_MEM)
    return o @ wo


def swiglu(x, wg, wu, wd):
    return (jax.nn.silu(x @ wg) * (x @ wu)) @ wd


def setup_inputs(seed: int = 0) -> dict:
    key = jax.random.key(seed)
    keys = iter(jax.random.split(key, 48))
    f32 = jnp.float32

    def nrm(shape, scale):
        return jax.random.normal(next(keys), shape, f32) * scale

    w_buf = min(WINDOW, PAST_LEN)
    d_in_even = D_S5 + D_SWA_Q + 2 * D_SWA_KV
    n_idx = jnp.arange(S5_STATE, dtype=f32)
    return {
        "x_prompt": nrm((BATCH, SEQ, D_MODEL), 1.0),
        "x_sample": nrm((DEC_BATCH, DEC_SEQ, D_MODEL), 1.0),
        "cache_mem_k": nrm((DEPTH, DEC_BATCH, N_MEM, MEM_HEADS, MEM_HEAD_DIM), 1.0),
        "cache_mem_v": nrm((DEPTH, DEC_BATCH, N_MEM, MEM_HEADS, MEM_HEAD_DIM), 1.0),
        "cache_swa_k": nrm((N_EVEN, DEC_BATCH, w_buf, SWA_KV_HEADS, HEAD_DIM), 1.0),
        "cache_swa_v": nrm((N_EVEN, DEC_BATCH, w_buf, SWA_KV_HEADS, HEAD_DIM), 1.0),
        "state_s5_re": nrm((N_EVEN, DEC_BATCH, S5_GROUPS, S5_STATE), 0.5),
        "state_s5_im": nrm((N_EVEN, DEC_BATCH, S5_GROUPS, S5_STATE), 0.5),
        "state_hgrn": nrm((N_ODD, DEC_BATCH, HG_HEADS, HG_DK, HG_DV), 0.5),
        "mem_prompt": nrm((BATCH, N_MEM, D_MODEL), 1.0),
        "rel_bias": nrm((N_BUCKETS, SWA_HEADS), 0.5),
        "w_even_in": nrm((N_EVEN, D_MODEL, d_in_even), D_MODEL ** -0.5),
        "s5_lam_re": -0.5 + nrm((N_EVEN, S5_GROUPS, S5_STATE), 0.01),
        "s5_lam_im": jnp.pi * n_idx + nrm((N_EVEN, S5_GROUPS, S5_STATE), 0.01),
        "s5_log_dt": jax.random.uniform(next(keys), (N_EVEN, S5_GROUPS), f32,
                                        minval=math.log(1e-3), maxval=math.log(1e-1)),
        "s5_b_re": nrm((N_EVEN, S5_GROUPS, S5_STATE, S5_GROUP), (2 * S5_GROUP) ** -0.5),
        "s5_b_im": nrm((N_EVEN, S5_GROUPS, S5_STATE, S5_GROUP), (2 * S5_GROUP) ** -0.5),
        "s5_c_re": nrm((N_EVEN, S5_GROUPS, S5_GROUP, S5_STATE), S5_STATE ** -0.5),
        "s5_c_im": nrm((N_EVEN, S5_GROUPS, S5_GROUP, S5_STATE), S5_STATE ** -0.5),
        "s5_d": nrm((N_EVEN, S5_GROUPS, S5_GROUP), 1.0),
        "s5_w_glu": nrm((N_EVEN, D_S5, D_S5), D_S5 ** -0.5),
        "swa_sinks": nrm((N_EVEN, SWA_HEADS), 1.0),
        "w_even_out": nrm((N_EVEN, D_MODEL, D_MODEL), BETA * D_MODEL ** -0.5),
        "hg_lb_logits": nrm((DEPTH, D_MODEL), 0.5),
        "w_odd_in": nrm((N_ODD, D_MODEL, 4 * D_MODEL), D_MODEL ** -0.5),
        "hg_norm_g": 1.0 + nrm((N_ODD, D_MODEL), 0.02),
        "w_odd_out": nrm((N_ODD, D_MODEL, D_MODEL), BETA * D_MODEL ** -0.5),
        "w_mem_q": nrm((DEPTH, D_MODEL, D_MEM), D_MODEL ** -0.5),
        "w_mem_k": nrm((DEPTH, D_MODEL, D_MEM), D_MODEL ** -0.5),
        "w_mem_v": nrm((DEPTH, D_MODEL, D_MEM), D_MODEL ** -0.5),
        "w_mem_o": nrm((DEPTH, D_MEM, D_MODEL), BETA * D_MEM ** -0.5),
        "w_ffn_gate": nrm((DEPTH, D_MODEL, D_FF), D_MODEL ** -0.5),
        "w_ffn_up": nrm((DEPTH, D_MODEL, D_FF), D_MODEL ** -0.5),
        "w_ffn_down": nrm((DEPTH, D_FF, D_MODEL), BETA * D_FF ** -0.5),
        "ln_g": 1.0 + nrm((DEPTH, 3, D_MODEL), 0.02),
        "ln_b": nrm((DEPTH, 3, D_MODEL), 0.02),
    }


def reference(x_prompt, x_sample, cache_mem_k, cache_mem_v, cache_swa_k, cache_swa_v,
              state_s5_re, state_s5_im, state_hgrn, mem_prompt, rel_bias,
              w_even_in, s5_lam_re, s5_lam_im, s5_log_dt, s5_b_re, s5_b_im, s5_c_re, s5_c_im,
              s5_d, s5_w_glu, swa_sinks, w_even_out, hg_lb_logits, w_odd_in, hg_norm_g, w_odd_out,
              w_mem_q, w_mem_k, w_mem_v, w_mem_o, w_ffn_gate, w_ffn_up, w_ffn_down, ln_g, ln_b):
    f32 = jnp.float32
    w_buf = cache_swa_k.shape[2]
    lb_soft = jax.nn.softmax(hg_lb_logits.astype(f32), axis=0)
    lower_bounds = jnp.cumsum(lb_soft, axis=0) - lb_soft[0]

    yp, ys = x_prompt, x_sample
    p_mem_k, p_mem_v = [], []
    p_swa_k, p_swa_v, p_s5_re, p_s5_im, p_hg = [], [], [], [], []
    s_swa_k, s_swa_v, s_s5_re, s_s5_im, s_hg = [], [], [], [], []

    for l in range(DEPTH):
        j = l // 2
        if l % 2 == 0:
            a_re, a_im, bb_re, bb_im = s5_discretize(s5_lam_re[j], s5_lam_im[j], s5_log_dt[j],
                                                     s5_b_re[j], s5_b_im[j])
            s5p = (a_re, a_im, bb_re, bb_im, s5_c_re[j], s5_c_im[j], s5_d[j], s5_w_glu[j])
            h0 = jnp.zeros((yp.shape[0], S5_GROUPS, S5_STATE), f32)
            mix_p, hr, hi, nk, nv = even_mixer(yp, h0, h0, None, None, w_even_in[j], s5p,
                                               swa_sinks[j], rel_bias, w_even_out[j], w_buf)
            p_s5_re.append(hr); p_s5_im.append(hi); p_swa_k.append(nk); p_swa_v.append(nv)
            mix_s, hr, hi, nk, nv = even_mixer(ys, state_s5_re[j], state_s5_im[j], cache_swa_k[j],
                                               cache_swa_v[j], w_even_in[j], s5p, swa_sinks[j],
                                               rel_bias, w_even_out[j], w_buf)
            s_s5_re.append(hr); s_s5_im.append(hi); s_swa_k.append(nk); s_swa_v.append(nv)
        else:
            s0 = jnp.zeros((yp.shape[0], HG_HEADS, HG_DK, HG_DV), f32)
            mix_p, sp = odd_mixer(yp, s0, lower_bounds[l], w_odd_in[j], hg_norm_g[j], w_odd_out[j])
            p_hg.append(sp)
            mix_s, ss = odd_mixer(ys, state_hgrn[j], lower_bounds[l], w_odd_in[j], hg_norm_g[j], w_odd_out[j])
            s_hg.append(ss)
        yp = layer_norm(ALPHA * yp + mix_p, ln_g[l, 0], ln_b[l, 0])
        ys = layer_norm(ALPHA * ys + mix_s, ln_g[l, 0], ln_b[l, 0])

        mk, mv = memory_kv(mem_prompt, w_mem_k[l], w_mem_v[l])
        p_mem_k.append(mk); p_mem_v.append(mv)
        yp = layer_norm(ALPHA * yp + cross_attn(yp, mk, mv, w_mem_q[l], w_mem_o[l]), ln_g[l, 1], ln_b[l, 1])
        ys = layer_norm(ALPHA * ys + cross_attn(ys, cache_mem_k[l], cache_mem_v[l], w_mem_q[l], w_mem_o[l]),
                        ln_g[l, 1], ln_b[l, 1])

        yp = layer_norm(ALPHA * yp + swiglu(yp, w_ffn_gate[l], w_ffn_up[l], w_ffn_down[l]), ln_g[l, 2], ln_b[l, 2])
        ys = layer_norm(ALPHA * ys + swiglu(ys, w_ffn_gate[l], w_ffn_up[l], w_ffn_down[l]), ln_g[l, 2], ln_b[l, 2])

    return (yp, ys,
            jnp.stack(p_mem_k), jnp.stack(p_mem_v),
            jnp.stack(p_swa_k), jnp.stack(p_swa_v),
            jnp.stack(p_s5_re), jnp.stack(p_s5_im), jnp.stack(p_hg),
            jnp.stack(s_swa_k), jnp.stack(s_swa_v),
            jnp.stack(s_s5_re), jnp.stack(s_s5_im), jnp.stack(s_hg))
```

```python
import math
import numpy as np
import concourse.bass as bass
import concourse.mybir as mybir
from concourse.bass_utils import run_bass_kernel_spmd

F32 = mybir.dt.float32
BF16 = mybir.dt.bfloat16
AF = mybir.ActivationFunctionType
ALU = mybir.AluOpType
AX = mybir.AxisListType

D = 4096
KT = 32
TP = 512
NPASS = 4
NSQ = 4
SC = 32
TMAX = TP + SC
DFF = 11008
FT = DFF // 128
ALPHA = 4.0 ** 0.25
MAGIC = 12582912.0
TWO_PI = 2.0 * math.pi
FF_ROUNDS = [(0, 32), (32, 64), (64, 86)]
DBG = {}


def scol(q, t=0):
    return TP + 8 * q + 4 + t


class Sem:
    def __init__(self, nc, name):
        self.h = nc.alloc_semaphore(name)
        self.n = 0


class Chain:
    def __init__(self, e, sem):
        self.e, self.sem, self.pending = e, sem, None

    def __getattr__(self, name):
        f = getattr(self.e, name)

        def call(*a, **k):
            if self.pending is not None:
                self.pending.then_inc(self.sem.h, 1)
                self.sem.n += 1
                self.e.wait_ge(self.sem.h, self.sem.n)
            ins = f(*a, **k)
            self.pending = ins
            return ins
        return call

    def finish(self, sst):
        self.pending.then_inc(sst.h, 1)


class Last:
    def __init__(self, e):
        self.e, self.last = e, None

    def __getattr__(self, name):
        f = getattr(self.e, name)

        def call(*a, **k):
            ins = f(*a, **k)
            self.last = ins
            return ins
        return call


class Builder:
    def __init__(self):
        self.nc = bass.Bass("TRN2", target_bir_lowering=False)
        self.din = {}
        self.dout = {}
        self.out_sem = None
        self.nsem = 0
        self.stages = []

    def inp(self, name, shape, dt=F32):
        if DBG.get('inputs') is not None and name not in DBG['inputs']:
            return None
        self.din[name] = self.nc.dram_tensor(name, list(shape), dt, kind="ExternalInput").ap()
        return self.din[name]

    def outp(self, name, shape):
        if DBG.get('outputs') is not None and name not in DBG['outputs']:
            return None
        self.dout[name] = self.nc.dram_tensor(name, list(shape), F32, kind="ExternalOutput").ap()
        return self.dout[name]

    def sb(self, name, shape, dt):
        return self.nc.alloc_sbuf_tensor(name, list(shape), dt)

    def sem(self, name):
        self.nsem += 1
        return Sem(self.nc, name)

    def blk(self, **fns):
        with self.nc.Block() as b:
            for k, fn in fns.items():
                getattr(b, k)(fn)

    def _rec(self, eng, fn, kind):
        self.stages.append((eng, fn, kind))

    def V(self, fn):
        self._rec("vector", fn, "chain")

    def A(self, fn):
        self._rec("scalar", fn, "chain")

    def P(self, fn):
        self._rec("tensor", fn, "last")

    def _dma(self, sem, pairs, eng, kw):
        sem.n += 16 * len(pairs)
        tgt = sem.n

        def f(e):
            for o, i in pairs:
                e.dma_start(out=o, in_=i, **kw).then_inc(sem.h, 16)
            e.wait_ge(sem.h, tgt)
        self._rec(eng, f, "dma")

    def dma(self, pairs, eng="sync", **kw):
        self._dma(self.dsem, pairs, eng, kw)

    def outdma(self, pairs):
        self._dma(self.osem, pairs, "sync", {})

    def flush(self):
        if not self.stages:
            return
        st, self.stages = self.stages, []
        base = self.sst.n
        by_eng = {}
        for k, (eng, fn, kind) in enumerate(st):
            by_eng.setdefault(eng, []).append((k, fn, kind))
        fns = {}
        for eng, lst in by_eng.items():
            def run(e, lst=lst, eng=eng):
                for (k, fn, kind) in lst:
                    if k > 0:
                        e.wait_ge(self.sst.h, base + k)
                    if kind == "chain":
                        c = Chain(e, self.svc if eng == "vector" else self.sac)
                        fn(c)
                        c.finish(self.sst)
                    elif kind == "last":
                        c = Last(e)
                        fn(c)
                        c.last.then_inc(self.sst.h, 1)
                    else:
                        fn(e)
                        e.sem_inc(self.sst.h, 1)
            fns[eng] = run
        self.blk(**fns)
        self.sst.n += len(st)

    def linear(self, w, chunks, ktw, act, ntiles, epi, epi_eng="scalar"):
        self.flush()
        sw, smm, sev = self.sw, self.smm, self.sev
        bw = [sw[0].n, sw[1].n]
        bmm, bev = smm.n, sev.n
        n = len(chunks)
        psA, wbuf = self.psA, self.wbuf

        def g(e):
            for i, j in enumerate(chunks):
                if i >= 2:
                    e.wait_ge(smm.h, bmm + i - 1)
                e.dma_start(out=wbuf[:, i % 2, :ktw, :], in_=w[j]).then_inc(sw[i % 2].h, 16)

        def pe(e):
            for i, j in enumerate(chunks):
                e.wait_ge(sw[i % 2].h, bw[i % 2] + 16 * (i // 2 + 1))
                if i >= 2:
                    e.wait_ge(sev.h, bev + i - 1)
                ins = None
                o = (i % 2) * 1024
                for (c0, cn) in ntiles:
                    for kt in range(ktw):
                        ins = e.matmul(psA[:, o + c0:o + c0 + cn], lhsT=wbuf[:, i % 2, kt, :],
                                       rhs=act(kt, c0, cn), start=(kt == 0), stop=(kt == ktw - 1))
                ins.then_inc(smm.h, 1)

        def ev(e):
            for i, j in enumerate(chunks):
                e.wait_ge(smm.h, bmm + i + 1)
                o = (i % 2) * 1024
                last = epi(e, i, j, lambda c0, cn, o=o: psA[:, o + c0:o + c0 + cn])
                last.then_inc(sev.h, 1)

        self.blk(**{"gpsimd": g, "tensor": pe, epi_eng: ev})
        sw[0].n += 16 * ((n + 1) // 2)
        sw[1].n += 16 * (n // 2)
        smm.n += n
        sev.n += n

    def epi_copy(self, dst, nt, func=AF.Copy, off=0):
        def epi(e, i, j, ps):
            last = None
            for (c0, cn) in nt:
                last = e.activation(out=dst[:, off + i, c0:c0 + cn], in_=ps(c0, cn), func=func)
            return last
        return epi

    def epi_res(self, first, nt):
        xres = self.xres

        def epi(e, i, j, ps):
            last = None
            for (c0, cn) in nt:
                if first:
                    last = e.scalar_tensor_tensor(out=xres[:, j, c0:c0 + cn], in0=xres[:, j, c0:c0 + cn],
                                                  scalar=ALPHA, in1=ps(c0, cn), op0=ALU.mult, op1=ALU.add)
                else:
                    last = e.tensor_tensor(out=xres[:, j, c0:c0 + cn], in0=xres[:, j, c0:c0 + cn],
                                           in1=ps(c0, cn), op=ALU.add)
            return last
        return epi

    def xact(self, buf):
        return lambda kt, c0, cn: buf[:, kt, c0:c0 + cn]

    def layernorm(self, T, nt, idx):
        xres, xb, actA, psS, st1 = self.xres, self.xb, self.actA, self.psS, self.st1

        def stat(e):
            for (c0, cn) in nt:
                for j in range(KT):
                    e.matmul(psS[:, c0:c0 + cn], lhsT=self.onesB[:, :], rhs=actA[:, j, c0:c0 + cn],
                             start=(j == 0), stop=(j == KT - 1))
        self.A(lambda e: e.activation(out=actA[:, :, :T], in_=xres[:, :, :T], func=AF.Copy))
        self.P(stat)

        def v1(e):
            e.tensor_scalar(out=st1[:, :T], in0=psS[:, :T], scalar1=1.0 / D, scalar2=None, op0=ALU.mult)
            e.tensor_tensor(out=xres[:, :, :T], in0=xres[:, :, :T],
                            in1=st1[:, :T].unsqueeze(1).to_broadcast([128, KT, T]), op=ALU.subtract)
        self.V(v1)
        self.A(lambda e: e.activation(out=actA[:, :, :T], in_=xres[:, :, :T], func=AF.Square))
        self.P(stat)
        self.V(lambda e: e.tensor_scalar(out=st1[:, :T], in0=psS[:, :T], scalar1=1.0 / D, scalar2=1e-5,
                                         op0=ALU.mult, op1=ALU.add))
        self.A(lambda e: e.activation(out=st1[:, :T], in_=st1[:, :T], func=AF.Sqrt))

        def v2(e):
            e.reciprocal(out=st1[:, :T], in_=st1[:, :T])
            e.tensor_tensor(out=xres[:, :, :T], in0=xres[:, :, :T],
                            in1=st1[:, :T].unsqueeze(1).to_broadcast([128, KT, T]), op=ALU.mult)
            gsl = self.lng[:, idx * KT:(idx + 1) * KT].unsqueeze(2).to_broadcast([128, KT, T])
            bsl = self.lnb[:, idx * KT:(idx + 1) * KT].unsqueeze(2).to_broadcast([128, KT, T])
            e.tensor_tensor(out=xres[:, :, :T], in0=xres[:, :, :T], in1=gsl, op=ALU.mult)
            e.tensor_tensor(out=xres[:, :, :T], in0=xres[:, :, :T], in1=bsl, op=ALU.add)
        self.V(v2)
        self.A(lambda e: e.activation(out=xb[:, :, :T], in_=xres[:, :, :T], func=AF.Copy))

    def ffn(self, T, nt, l):
        actA = self.actA
        for r, (a, b) in enumerate(FF_ROUNDS):
            self.linear(self.din["w_gate"][l], list(range(a, b)), KT, self.xact(self.xb), nt,
                        self.epi_copy(actA, nt, AF.Silu), "scalar")

            def epi_u(e, i, j, ps):
                last = None
                for (c0, cn) in nt:
                    last = e.tensor_tensor(out=actA[:, i, c0:c0 + cn], in0=actA[:, i, c0:c0 + cn],
                                           in1=ps(c0, cn), op=ALU.mult)
                return last
            self.linear(self.din["w_up"][l], list(range(a, b)), KT, self.xact(self.xb), nt, epi_u, "vector")
            self.linear(self.din["w_down%d" % r][l], list(range(KT)), b - a, self.xact(actA), nt,
                        self.epi_res(r == 0, nt), "vector")

    def attn(self, M, q_ap, kf_list, vt_list, out_ps_rows, out_ap, scale, bias_ap=None, sink_ap=None, extra_mask=None):
        psT, psPT, psO = self.psT, self.psPT, self.psO
        sb, pn, ptb, mx, sm = self.at_s, self.at_p, self.at_pt, self.at_mx, self.at_sm
        nks = [k.shape[-1] for k in kf_list]
        NK = sum(nks)
        offs = [sum(nks[:i]) for i in range(len(nks))]

        def p1(e):
            for k, o in zip(kf_list, offs):
                e.matmul(psT[:M, o:o + k.shape[-1]], lhsT=q_ap, rhs=k, start=True, stop=True)
        self.P(p1)

        def v1(e):
            if bias_ap is not None:
                e.scalar_tensor_tensor(out=sb[:M, :NK], in0=psT[:M, :NK], scalar=scale, in1=bias_ap,
                                       op0=ALU.mult, op1=ALU.add)
            else:
                e.tensor_scalar(out=sb[:M, :NK], in0=psT[:M, :NK], scalar1=scale, scalar2=None, op0=ALU.mult)
            if extra_mask is not None:
                a, b = extra_mask
                e.tensor_scalar(out=sb[:M, a:b], in0=sb[:M, a:b], scalar1=-1e30, scalar2=None, op0=ALU.add)
            e.tensor_reduce(out=mx[:M, 0:1], in_=sb[:M, :NK], axis=AX.X, op=ALU.max)
            if sink_ap is not None:
                e.tensor_tensor(out=mx[:M, 0:1], in0=mx[:M, 0:1], in1=sink_ap, op=ALU.max)
            e.tensor_scalar(out=mx[:M, 1:2], in0=mx[:M, 0:1], scalar1=-1.0, scalar2=None, op0=ALU.mult)
        self.V(v1)

        def a1(e):
            e.activation(out=sb[:M, :NK], in_=sb[:M, :NK], func=AF.Exp, bias=mx[:M, 1:2], scale=1.0)
            if sink_ap is not None:
                e.activation(out=sm[:M, 1:2], in_=sink_ap, func=AF.Exp, bias=mx[:M, 1:2], scale=1.0)
        self.A(a1)

        def v2(e):
            e.tensor_reduce(out=sm[:M, 0:1], in_=sb[:M, :NK], axis=AX.X, op=ALU.add)
            if sink_ap is not None:
                e.tensor_tensor(out=sm[:M, 0:1], in0=sm[:M, 0:1], in1=sm[:M, 1:2], op=ALU.add)
            e.reciprocal(out=sm[:M, 0:1], in_=sm[:M, 0:1])
            e.tensor_scalar(out=pn[:M, :NK], in0=sb[:M, :NK], scalar1=sm[:M, 0:1], scalar2=None, op0=ALU.mult)
        self.V(v2)

        def p2(e):
            for i, (nk, o) in enumerate(zip(nks, offs)):
                e.transpose(out=psPT[:nk, i * 128:i * 128 + M], in_=pn[:M, o:o + nk], identity=self.identB[:M, :M])
        self.P(p2)

        def v3(e):
            for i, nk in enumerate(nks):
                e.tensor_copy(out=ptb[:nk, i, :M], in_=psPT[:nk, i * 128:i * 128 + M])
        self.V(v3)
        r0, dh = out_ps_rows

        def p3(e):
            for i, (v, nk) in enumerate(zip(vt_list, nks)):
                e.matmul(psO[r0:r0 + dh, :M], lhsT=v, rhs=ptb[:nk, i, :M], start=(i == 0), stop=(i == len(nks) - 1))
        self.P(p3)
        self.A(lambda e: e.activation(out=out_ap, in_=psO[r0:r0 + dh, :M], func=AF.Copy))

    def tr32(self, e, src, rows, cols, slot=0):
        e.transpose(out=self.psTr[:cols, slot * 128:slot * 128 + rows], in_=src, identity=self.identF[:rows, :rows])

    def mem_kv(self, l, write_out=True):
        nt = [(0, 256)]
        memT = self.actA
        self.dma([(memT[:, :, 0:256], self.din["memT"].rearrange("j p t -> p j t"))], eng="gpsimd")
        f32 = self.f32tmp

        def epi_k(e, i, j, ps):
            e.activation(out=f32[:, i, 0:256], in_=ps(0, 256), func=AF.Copy)
            return e.activation(out=self.mkF[:, i, :], in_=ps(0, 256), func=AF.Copy)
        self.linear(self.din["w_mk"][l], list(range(4)), KT, self.xact(memT), nt, epi_k, "scalar")

        def epi_v(e, i, j, ps):
            return e.activation(out=f32[:, 4 + i, 0:256], in_=ps(0, 256), func=AF.Copy)
        self.linear(self.din["w_mv"][l], list(range(4)), KT, self.xact(memT), nt, epi_v, "scalar")
        for which, outname in ((0, "p_mem_k"), (1, "p_mem_v")):
            for blk in range(2):
                def pt(e, which=which, blk=blk):
                    for hd in range(4):
                        self.tr32(e, f32[:, which * 4 + hd, blk * 128:(blk + 1) * 128], 128, 128, slot=hd)
                self.P(pt)

                def cp(e, which=which, blk=blk):
                    e.tensor_copy(out=self.tm32[:, 0:512], in_=self.psTr[:, 0:512])
                    if which == 1:
                        e.tensor_copy(out=self.mvT[:, blk, :], in_=self.psTr[:, 0:512])
                self.V(cp)
                if write_out:
                    self.outdma([(self.dout[outname][l, blk * 128:(blk + 1) * 128, :], self.tm32[:, 0:512])])

    def cross(self, T, nt, l, ps_idx):
        qF = self.actB
        oF = self.actA
        sc = 128 ** -0.5
        self.linear(self.din["w_mq"][l], list(range(4)), KT, self.xact(self.xb), nt, self.epi_copy(qF, nt), "scalar")
        if ps_idx == 0:
            self.V(lambda e: e.memset(oF[:, 0:4, TP:TMAX], 0.0))
        for tb in range(TP // 128):
            for hd in range(4):
                self.attn2(qF, oF, l, hd, tb, sc)
        if ps_idx == 0:
            for q in range(NSQ):
                self.cross_sample(qF, oF, l, q, sc)
        self.linear(self.din["w_mo"][l], list(range(KT)), 4, self.xact(oF), nt, self.epi_res(True, nt), "vector")

    def attn2(self, qF, oF, l, hd, tb, sc):
        kf = [self.mkF[:, hd, 0:128], self.mkF[:, hd, 128:256]]
        vt = [self.mvT[:, b, hd * 128:(hd + 1) * 128] for b in range(2)]
        self.attn(128, qF[:, hd, tb * 128:(tb + 1) * 128], kf, vt, (0, 128), oF[:, hd, tb * 128:(tb + 1) * 128], sc)

    def cross_sample(self, qF, oF, l, q, sc):
        ck = self.din["c_mem_k"][l, q].rearrange("(b t) f -> t b f", t=128)
        cv = self.din["c_mem_v"][l, q].rearrange("(b t) f -> t b f", t=128)
        self.dma([(self.ck32[:, :, :], ck), (self.cv32[:, :, :], cv)])
        for b in range(2):
            def pt(e, b=b):
                for hd in range(4):
                    self.tr32(e, self.ck32[:, b, hd * 128:(hd + 1) * 128], 128, 128, slot=hd)
            self.P(pt)
            self.V(lambda e, b=b: e.tensor_copy(out=self.ckF[:, :, b * 128:(b + 1) * 128],
                                               in_=self.psTr[:, 0:512].rearrange("p (h t) -> p h t", h=4)))
        self.V(lambda e: e.tensor_copy(out=self.cvT[:, :, :], in_=self.cv32[:, :, :]))
        c0 = scol(q)
        for hd in range(4):
            kf = [self.ckF[:, hd, 0:128], self.ckF[:, hd, 128:256]]
            vt = [self.cvT[:, b, hd * 128:(hd + 1) * 128] for b in range(2)]
            self.attn(4, qF[:, hd, c0:c0 + 4], kf, vt, (0, 128), oF[:, hd, c0:c0 + 4], sc)


    def ar_reset(self, keep_common=True):
        self.ar_o = self.AR_COMMON if keep_common else 0

    def ar_f32(self, *shape):
        n = int(np.prod(shape))
        v = self.scr[:, self.ar_o:self.ar_o + n]
        self.ar_o += n
        assert self.ar_o <= self.AR_WORDS, ("arena overflow", self.ar_o)
        return self._shape(v, shape)

    def ar_bf(self, *shape):
        n = int(np.prod(shape))
        w = (n + 1) // 2
        v = self.scr[:, self.ar_o:self.ar_o + w].bitcast(BF16)[:, :n]
        self.ar_o += w
        assert self.ar_o <= self.AR_WORDS, ("arena overflow", self.ar_o)
        return self._shape(v, shape)

    @staticmethod
    def _shape(v, shape):
        if len(shape) == 1:
            return v
        if len(shape) == 2:
            return v.rearrange("p (a b) -> p a b", a=shape[0])
        return v.rearrange("p (a b c) -> p a b c", a=shape[0], b=shape[1])

    def carve_cross(self):
        self.ar_reset()
        self.f32tmp = self.ar_f32(8, 256)
        self.ck32 = self.f32tmp[:, 0:4, :].rearrange("p (a b) c -> p a (b c)", a=2)
        self.cv32 = self.f32tmp[:, 4:8, :].rearrange("p (a b) c -> p a (b c)", a=2)
        self.tm32 = self.ar_f32(512)
        self.ckF = self.ar_bf(4, 256)
        self.cvT = self.ar_bf(2, 512)
        self.actB = self.ar_bf(4, TMAX)

    def rred(self, e, x, tmp):
        e.tensor_scalar(out=tmp, in0=x, scalar1=1.0 / TWO_PI, scalar2=MAGIC, op0=ALU.mult, op1=ALU.add)
        e.tensor_scalar(out=tmp, in0=tmp, scalar1=MAGIC, scalar2=-TWO_PI, op0=ALU.subtract, op1=ALU.mult)
        e.tensor_tensor(out=x, in0=x, in1=tmp, op=ALU.add)
        e.tensor_scalar(out=x, in0=x, scalar1=3.141592, scalar2=-3.141592, op0=ALU.min, op1=ALU.max)

    def s5_setup(self):
        d = self.din
        nc = self.nc
        self.cst_d = nc.dram_tensor("cst_d", [16, 128, 1024], BF16).ap()
        self.bst_d = nc.dram_tensor("bst_d", [16, 128, 1024], BF16).ap()
        self.ar_reset(False)
        A = self.ar_f32
        lre, lim, ldt = A(64), A(64), A(64)
        bre, bim, cre, cim = A(64, 16), A(64, 16), A(64, 16), A(64, 16)
        Bre, Bim = A(64, 16), A(64, 16)
        lr, dt_, t1, th, thc, sn, cs, are, aim, den, am1, fr, fi, nfi = [A(64) for _ in range(14)]
        self.dma([(lre, d["s5_lre"]), (lim, d["s5_lim"]), (ldt, d["s5_ldt"]), (bre, d["s5_bre"]),
                  (bim, d["s5_bim"]), (cre, d["s5_cre"]), (cim, d["s5_cim"]), (self.s5_dA[:, :], d["s5_dA"])])
        mag, thr = self.s5_mag[:, :], self.s5_thr[:, :]
        self.V(lambda e: e.tensor_scalar(out=lr, in0=lre, scalar1=-1e-4, scalar2=None, op0=ALU.min))
        self.A(lambda e: e.activation(out=dt_, in_=ldt, func=AF.Exp))
        self.V(lambda e: e.tensor_tensor(out=t1, in0=lr, in1=dt_, op=ALU.mult))
        self.A(lambda e: e.activation(out=mag, in_=t1, func=AF.Exp))

        def v1(e):
            e.tensor_tensor(out=thr, in0=lim, in1=dt_, op=ALU.mult)
            self.rred(e, thr, t1)
            e.tensor_scalar(out=thc, in0=thr, scalar1=math.pi / 2, scalar2=None, op0=ALU.add)
            self.rred(e, thc, t1)
        self.V(v1)

        def a1(e):
            e.activation(out=sn, in_=thr, func=AF.Sin)
            e.activation(out=cs, in_=thc, func=AF.Sin)
        self.A(a1)
        TT = lambda e, o, a, b, op: e.tensor_tensor(out=o, in0=a, in1=b, op=op)

        def v2(e):
            TT(e, are, mag, cs, ALU.mult); TT(e, aim, mag, sn, ALU.mult)
            TT(e, den, lr, lr, ALU.mult); TT(e, t1, lim, lim, ALU.mult); TT(e, den, den, t1, ALU.add)
            e.reciprocal(out=den, in_=den)
            e.tensor_scalar(out=am1, in0=are, scalar1=-1.0, scalar2=None, op0=ALU.add)
            TT(e, fr, am1, lr, ALU.mult); TT(e, t1, aim, lim, ALU.mult); TT(e, fr, fr, t1, ALU.add); TT(e, fr, fr, den, ALU.mult)
            TT(e, fi, aim, lr, ALU.mult); TT(e, t1, am1, lim, ALU.mult); TT(e, fi, fi, t1, ALU.subtract); TT(e, fi, fi, den, ALU.mult)
            e.tensor_scalar(out=nfi, in0=fi, scalar1=-1.0, scalar2=None, op0=ALU.mult)
            for pr in range(64):
                e.tensor_scalar(out=Bre[:, pr, :], in0=bre[:, pr, :], scalar1=fr[:, pr:pr + 1], scalar2=None, op0=ALU.mult)
                e.scalar_tensor_tensor(out=Bre[:, pr, :], in0=bim[:, pr, :], scalar=nfi[:, pr:pr + 1], in1=Bre[:, pr, :],
                                       op0=ALU.mult, op1=ALU.add)
                e.tensor_scalar(out=Bim[:, pr, :], in0=bim[:, pr, :], scalar1=fr[:, pr:pr + 1], scalar2=None, op0=ALU.mult)
                e.scalar_tensor_tensor(out=Bim[:, pr, :], in0=bre[:, pr, :], scalar=fi[:, pr:pr + 1], in1=Bim[:, pr, :],
                                       op0=ALU.mult, op1=ALU.add)
            e.tensor_scalar(out=cim, in0=cim, scalar1=-1.0, scalar2=None, op0=ALU.mult)
        self.V(v2)
        Ct = self.ar_bf(4, 2, 128)
        Bt = self.ar_bf(4, 2, 128)
        Zp = A(4, 128)
        for ft in range(16):
            def vc(e, ft=ft):
                e.memset(Ct, 0.0)
                for k in range(4):
                    for ee in range(2):
                        r = slice(ee * 64, ee * 64 + 64)
                        c0 = (2 * k + ee) * 16
                        e.tensor_copy(out=Ct[r, k, 0, c0:c0 + 16], in_=cre[r, 4 * ft + k, :])
                        e.tensor_copy(out=Ct[r, k, 1, c0:c0 + 16], in_=cim[r, 4 * ft + k, :])
            self.V(vc)
            for c, Bc in enumerate((Bre, Bim)):
                def vz(e, ft=ft, Bc=Bc):
                    e.memset(Zp, 0.0)
                    for k in range(4):
                        for ee in range(2):
                            r = slice(ee * 64, ee * 64 + 64)
                            c0 = (2 * k + ee) * 16
                            e.tensor_copy(out=Zp[r, k, c0:c0 + 16], in_=Bc[r, 4 * ft + k, :])
                self.V(vz)

                def pz(e):
                    for k in range(4):
                        self.tr32(e, Zp[:, k, :], 128, 128, slot=k)
                self.P(pz)
                self.V(lambda e, c=c: e.tensor_copy(out=Bt[:, :, c, :],
                                                    in_=self.psTr[:, 0:512].rearrange("p (k m) -> p k m", k=4)))
            self.dma([(self.cst_d[ft], Ct.rearrange("p a b c -> p (a b c)")),
                      (self.bst_d[ft], Bt.rearrange("p a b c -> p (a b c)"))])

    def s5_pass(self, ps_idx, T, nt):
        d = self.din
        has_s = ps_idx == 0
        self.ar_reset()
        Bt = self.ar_bf(4, 2, 128)
        Ct = self.ar_bf(4, 2, 128)
        iota1 = self.ar_f32(512)
        Ctab, Stab = self.ar_f32(512), self.ar_f32(512)
        Cs, Ss = self.ar_f32(32), self.ar_f32(32)
        Wr, Wi, Gr, Gi, tmp = [self.ar_f32(TMAX) for _ in range(5)]
        Hbf = self.ar_bf(4, 2, TMAX)
        h0re, h0im = self.ar_f32(64, NSQ), self.ar_f32(64, NSQ)
        loads = [(iota1, d["iota1"])]
        if has_s:
            loads += [(h0re, d["s5_h0re"]), (h0im, d["s5_h0im"])]
        self.dma(loads)
        mag, thr, Hpre, Hpim = self.s5_mag, self.s5_thr, self.s5_Hre, self.s5_Him
        psA = self.psA
        TT = lambda e, o, a, b, op: e.tensor_tensor(out=o, in0=a, in1=b, op=op)

        def vinit(e):
            e.memset(Gr, 0.0); e.memset(Gi, 0.0)
            if ps_idx == 0:
                e.memset(Hpre[:, :], 0.0); e.memset(Hpim[:, :], 0.0)
        self.V(vinit)
        for ft in range(16):
            self.dma([(Bt.rearrange("p a b c -> p (a b c)"), self.bst_d[ft]),
                      (Ct.rearrange("p a b c -> p (a b c)"), self.cst_d[ft])])
            u = self.actA[:, 16 + ft, :]
            for k in range(4):
                pr = 4 * ft + k

                def pv(e, k=k, u=u):
                    for c in range(2):
                        for (c0, cn) in nt:
                            e.matmul(psA[:, c * 1024 + c0:c * 1024 + c0 + cn], lhsT=Bt[:, k, c, :], rhs=u[:, c0:c0 + cn],
                                     start=True, stop=True)
                self.P(pv)

                def vph(e, pr=pr):
                    e.tensor_scalar(out=Wr[:, 0:512], in0=iota1, scalar1=thr[:, pr:pr + 1], scalar2=None, op0=ALU.mult)
                    self.rred(e, Wr[:, 0:512], tmp[:, 0:512])
                    e.tensor_scalar(out=Wi[:, 0:512], in0=Wr[:, 0:512], scalar1=math.pi / 2, scalar2=None, op0=ALU.add)
                    self.rred(e, Wi[:, 0:512], tmp[:, 0:512])
                self.V(vph)

                def asin(e):
                    e.activation(out=Stab, in_=Wr[:, 0:512], func=AF.Sin)
                    e.activation(out=Ctab, in_=Wi[:, 0:512], func=AF.Sin)
                self.A(asin)

                def vmain(e, pr=pr, k=k):
                    Vr, Vi = psA[:, 0:TMAX], psA[:, 1024:1024 + TMAX]
                    segs = [(slice(0, 512), Ctab, Stab)]
                    if has_s:
                        e.memset(Cs, 0.0); e.memset(Ss, 0.0)
                        for q in range(NSQ):
                            e.tensor_copy(out=Cs[:, 8 * q + 4:8 * q + 8], in_=Ctab[:, 0:4])
                            e.tensor_copy(out=Ss[:, 8 * q + 4:8 * q + 8], in_=Stab[:, 0:4])
                        segs.append((slice(512, TMAX), Cs, Ss))
                    for (sl, Cc, Sc) in segs:
                        TT(e, Wr[:, sl], Cc, Vr[:, sl], ALU.mult); TT(e, tmp[:, sl], Sc, Vi[:, sl], ALU.mult)
                        TT(e, Wr[:, sl], Wr[:, sl], tmp[:, sl], ALU.add)
                        TT(e, Wi[:, sl], Cc, Vi[:, sl], ALU.mult); TT(e, tmp[:, sl], Sc, Vr[:, sl], ALU.mult)
                        TT(e, Wi[:, sl], Wi[:, sl], tmp[:, sl], ALU.subtract)
                    mg = mag[:, pr:pr + 1]
                    e.tensor_tensor_scan(out=Gr[:, 0:512], data0=mg.to_broadcast([128, 512]), data1=Wr[:, 0:512],
                                         initial=Hpre[:, pr:pr + 1], op0=ALU.mult, op1=ALU.add)
                    e.tensor_tensor_scan(out=Gi[:, 0:512], data0=mg.to_broadcast([128, 512]), data1=Wi[:, 0:512],
                                         initial=Hpim[:, pr:pr + 1], op0=ALU.mult, op1=ALU.add)
                    if has_s:
                        for q in range(NSQ):
                            c0 = scol(q)
                            e.tensor_tensor_scan(out=Gr[:, c0:c0 + 4], data0=mg.to_broadcast([128, 4]), data1=Wr[:, c0:c0 + 4],
                                                 initial=h0re[:, pr, q:q + 1], op0=ALU.mult, op1=ALU.add)
                            e.tensor_tensor_scan(out=Gi[:, c0:c0 + 4], data0=mg.to_broadcast([128, 4]), data1=Wi[:, c0:c0 + 4],
                                                 initial=h0im[:, pr, q:q + 1], op0=ALU.mult, op1=ALU.add)
                    for (sl, Cc, Sc) in segs:
                        TT(e, Wr[:, sl], Cc, Gr[:, sl], ALU.mult); TT(e, tmp[:, sl], Sc, Gi[:, sl], ALU.mult)
                        TT(e, Wr[:, sl], Wr[:, sl], tmp[:, sl], ALU.subtract)
                        TT(e, Wi[:, sl], Cc, Gi[:, sl], ALU.mult); TT(e, tmp[:, sl], Sc, Gr[:, sl], ALU.mult)
                        TT(e, Wi[:, sl], Wi[:, sl], tmp[:, sl], ALU.add)
                    e.tensor_copy(out=Hpre[:, pr:pr + 1], in_=Wr[:, 511:512])
                    e.tensor_copy(out=Hpim[:, pr:pr + 1], in_=Wi[:, 511:512])
                    if has_s:
                        for q in range(NSQ):
                            c0 = scol(q) + 3
                            e.tensor_copy(out=self.s5_fre[:, q, pr:pr + 1], in_=Wr[:, c0:c0 + 1])
                            e.tensor_copy(out=self.s5_fim[:, q, pr:pr + 1], in_=Wi[:, c0:c0 + 1])
                    e.tensor_copy(out=Hbf[:, k, 0, :T], in_=Wr[:, :T])
                    e.tensor_copy(out=Hbf[:, k, 1, :T], in_=Wi[:, :T])
                self.V(vmain)

            def py(e):
                for (c0, cn) in nt:
                    n = 0
                    for k in range(4):
                        for c in range(2):
                            e.matmul(psA[:, c0:c0 + cn], lhsT=Ct[:, k, c, :], rhs=Hbf[:, k, c, c0:c0 + cn],
                                     start=(n == 0), stop=(n == 7))
                            n += 1
            self.P(py)
            self.V(lambda e, ft=ft, u=u: e.scalar_tensor_tensor(out=tmp[:, :T], in0=u[:, :T], scalar=self.s5_dA[:, ft:ft + 1],
                                                                in1=psA[:, :T], op0=ALU.mult, op1=ALU.add))
            self.A(lambda e, u=u: e.activation(out=u[:, :T], in_=tmp[:, :T], func=AF.Gelu_apprx_tanh))

    def swa_setup(self):
        d = self.din
        nc = self.nc
        self.Lx = nc.dram_tensor("Lx_d", [32, 384], F32).ap()
        self.biasD = nc.dram_tensor("bias_d", [32, 128, 256], F32).ap()
        self.ar_reset(False)
        A = self.ar_f32
        rb, oh, neg, Rsb, Ch, Bsb, Jm = A(32), A(128), A(384), A(32), A(256), A(256), A(128)
        self.dma([(rb[:32, :], d["rel_bias"]), (oh[:32, :], d["ohrev"]), (Jm, d["Jmat"]), (self.sinkb[:, :], d["sinkb"])])
        self.V(lambda e: e.memset(neg, -1e30))
        self.dma([(self.Lx, neg[:32, :])])
        self.P(lambda e: e.matmul(self.psT[:, 0:32], lhsT=oh[:32, :], rhs=rb[:32, :], start=True, stop=True))
        self.V(lambda e: e.tensor_copy(out=Rsb, in_=self.psT[:, 0:32]))
        self.dma([(self.Lx[:, 128:256].rearrange("h k -> k h"), Rsb)], allow_slow_non_contiguous=True)
        for h in range(32):
            hank = bass.AP(tensor=self.Lx.tensor, offset=h * 384, ap=[[1, 128], [1, 256]])
            self.dma([(Ch, hank)])
            self.P(lambda e: e.matmul(self.psT[:, 0:256], lhsT=Jm, rhs=Ch, start=True, stop=True))
            self.V(lambda e: e.tensor_copy(out=Bsb, in_=self.psT[:, 0:256]))
            self.dma([(self.biasD[h], Bsb)])

    def swa_pass(self, ps_idx, T, nt):
        d, actA = self.din, self.actA
        has_s = ps_idx == 0
        self.ar_reset()
        kFp = self.ar_bf(4, 128 + TMAX)
        vTp = self.ar_bf(5, 512)
        vF = self.ar_bf(4, TMAX)
        ck32, cv32 = self.ar_f32(512), self.ar_f32(512)
        f32k = self.ar_f32(8, 160)
        B = self.ar_f32(256)
        ckF = self.ar_bf(4, 128)
        cvT = self.ar_bf(512)
        knv32 = self.ar_f32(1024)
        vnT = self.ar_bf(512)

        def vh(e):
            if ps_idx == 0:
                e.memset(self.kh[:, :, :], 0.0); e.memset(self.vh[:, :], 0.0)
            e.tensor_copy(out=kFp[:, :, 0:128], in_=self.kh[:, :, :])
            e.tensor_copy(out=vTp[:, 0, :], in_=self.vh[:, :])
        self.V(vh)

        def epi_kv(e, i, j, ps):
            dst = kFp[:, i, 128:128 + TMAX] if i < 4 else vF[:, i - 4, :]
            for (c0, cn) in nt:
                e.activation(out=dst[:, c0:c0 + cn], in_=ps(c0, cn), func=AF.Copy)
            last = e.activation(out=f32k[:, i, 0:128], in_=ps(384, 128), func=AF.Copy)
            if has_s:
                last = e.activation(out=f32k[:, i, 128:160], in_=ps(512, 32), func=AF.Copy)
            return last
        self.linear(d["w_ein"], list(range(32, 40)), KT, self.xact(self.xb), nt, epi_kv, "scalar")
        for blk in range(4):
            def pt(e, blk=blk):
                for vc in range(4):
                    e.transpose(out=self.psPT[:, vc * 128:(vc + 1) * 128], in_=vF[:, vc, blk * 128:(blk + 1) * 128],
                                identity=self.identB[:, :])
            self.P(pt)
            self.V(lambda e, blk=blk: e.tensor_copy(out=vTp[:, 1 + blk, :], in_=self.psPT[:, 0:512]))
        self.linear(d["w_ein"], list(range(16, 32)), KT, self.xact(self.xb), nt,
                    self.epi_copy(actA, nt, AF.Copy, off=16), "scalar")

        def hinfo(c, s):
            j, i = c // 4, c % 4
            return j, 8 * j + i + 4 * s, 2 * j + s, slice(64 * s, 64 * s + 64)
        for c in range(16):
            for s in range(2):
                j, h, g, rows = hinfo(c, s)
                self.dma([(B, self.biasD[h])])
                for blk in range(4):
                    self.attn(128, actA[rows, 16 + c, blk * 128:(blk + 1) * 128],
                              [kFp[rows, j, blk * 128:blk * 128 + 128], kFp[rows, j, blk * 128 + 128:blk * 128 + 256]],
                              [vTp[:, blk, g * 64:(g + 1) * 64], vTp[:, blk + 1, g * 64:(g + 1) * 64]],
                              (64 * s, 64), actA[rows, 16 + c, blk * 128:(blk + 1) * 128], 0.125,
                              bias_ap=B, sink_ap=self.sinkb[:, h:h + 1],
                              extra_mask=(0, 128) if (ps_idx == 0 and blk == 0) else None)
        if has_s:
            for q in range(NSQ):
                self.dma([(ck32, d["c_swa_k"][q]), (cv32, d["c_swa_v"][q])])
                self.outdma([(self.dout["s_swa_k"][q, 0:124, :], d["c_swa_k"][q, 4:128, :]),
                             (self.dout["s_swa_v"][q, 0:124, :], d["c_swa_v"][q, 4:128, :])])

                def pk(e):
                    for vc in range(4):
                        self.tr32(e, ck32[:, vc * 128:(vc + 1) * 128], 128, 128, slot=vc)
                self.P(pk)

                def vk(e):
                    e.tensor_copy(out=ckF, in_=self.psTr[:, 0:512].rearrange("p (a b) -> p a b", a=4))
                    e.tensor_copy(out=cvT, in_=cv32)
                self.V(vk)
                cq = 128 + 8 * q + 4
                for half in range(2):
                    def pn_(e, half=half, cq=cq):
                        for vc in range(4):
                            self.tr32(e, f32k[:, half * 4 + vc, cq:cq + 4], 128, 4, slot=vc)
                    self.P(pn_)
                    self.V(lambda e, half=half: e.tensor_copy(out=knv32[:4, half * 512:(half + 1) * 512],
                                                              in_=self.psTr[:4, 0:512]))
                self.V(lambda e: e.tensor_copy(out=vnT[:4, :], in_=knv32[:4, 512:1024]))
                self.outdma([(self.dout["s_swa_k"][q, 124:128, :], knv32[:4, 0:512]),
                             (self.dout["s_swa_v"][q, 124:128, :], knv32[:4, 512:1024])])
                c0 = scol(q)
                for c in range(16):
                    for s in range(2):
                        j, h, g, rows = hinfo(c, s)
                        self.dma([(B[:4, 0:132], self.biasD[h, 0:4, 0:132])])
                        self.attn(4, actA[rows, 16 + c, c0:c0 + 4],
                                  [ckF[rows, j, :], kFp[rows, j, 128 + c0:128 + c0 + 4]],
                                  [cvT[:, g * 64:(g + 1) * 64], vnT[:4, g * 64:(g + 1) * 64]],
                                  (64 * s, 64), actA[rows, 16 + c, c0:c0 + 4], 0.125,
                                  bias_ap=B[:4, 0:132], sink_ap=self.sinkb[:4, h:h + 1])
        if ps_idx == NPASS - 1:
            for half, nm in ((0, "p_swa_k"), (1, "p_swa_v")):
                def pl(e, half=half):
                    for vc in range(4):
                        self.tr32(e, f32k[:, half * 4 + vc, 0:128], 128, 128, slot=vc)
                self.P(pl)
                self.V(lambda e: e.tensor_copy(out=ck32, in_=self.psTr[:, 0:512]))
                self.outdma([(self.dout[nm], ck32)])

        def vcarry(e):
            e.tensor_copy(out=self.kh[:, :, :], in_=kFp[:, :, 512:640])
            e.tensor_copy(out=self.vh[:, :], in_=vTp[:, 4, :])
        self.V(vcarry)

    def s5_outputs(self, ps_idx):
        self.ar_reset()
        tm = self.ar_f32(128)
        if ps_idx == 0:
            for src, nm in ((self.s5_fre, "s_s5_re"), (self.s5_fim, "s_s5_im")):
                for b in range(2):
                    self.P(lambda e, src=src, b=b: self.tr32(e, src[:, 2 * b:2 * b + 2, :].rearrange("p a b -> p (a b)"), 128, 128))
                    self.V(lambda e: e.tensor_copy(out=tm, in_=self.psTr[:, 0:128]))
                    self.outdma([(self.dout[nm][b * 128:(b + 1) * 128, :], tm)])
        if ps_idx == NPASS - 1:
            for src, nm in ((self.s5_Hre, "p_s5_re"), (self.s5_Him, "p_s5_im")):
                self.P(lambda e, src=src: self.tr32(e, src[:, :], 128, 64))
                self.V(lambda e: e.tensor_copy(out=tm[:64, :], in_=self.psTr[:64, 0:128]))
                self.outdma([(self.dout[nm], tm[:64, :])])

    def even_mixer(self, ps_idx, T, nt):
        d, actA = self.din, self.actA
        self.linear(d["w_ein"], list(range(16)), KT, self.xact(self.xb), nt,
                    self.epi_copy(actA, nt, AF.Copy, off=16), "scalar")
        self.s5_pass(ps_idx, T, nt)
        self.s5_outputs(ps_idx)
        self.linear(d["w_glu"], list(range(16)), 16, lambda kt, c0, cn: actA[:, 16 + kt, c0:c0 + cn], nt,
                    self.epi_copy(actA, nt, AF.Sigmoid, off=0), "scalar")
        self.V(lambda e: e.tensor_tensor(out=actA[:, 0:16, :T], in0=actA[:, 0:16, :T], in1=actA[:, 16:32, :T], op=ALU.mult))
        self.swa_pass(ps_idx, T, nt)
        self.linear(d["w_eout"], list(range(KT)), KT, self.xact(actA), nt, self.epi_res(True, nt), "vector")

    def hg_setup(self):
        d = self.din
        self.hst_d = self.nc.dram_tensor("hst_d", [32, 128, 128], F32).ap()
        self.ar_reset(False)
        l0, l1 = self.ar_f32(32), self.ar_f32(32)
        self.dma([(l0, d["hg_l0"]), (l1, d["hg_l1"]), (self.hg_gn[:, :], d["hg_gn"])])
        self.V(lambda e: e.tensor_tensor(out=l1, in0=l1, in1=l0, op=ALU.subtract))
        self.A(lambda e: e.activation(out=self.hg_lb[:, :], in_=l1, func=AF.Sigmoid))
        self.V(lambda e: e.tensor_scalar(out=self.hg_oml[:, :], in0=self.hg_lb[:, :], scalar1=-1.0, scalar2=1.0,
                                         op0=ALU.mult, op1=ALU.add))

    def odd_mixer(self, ps_idx, T, nt):
        d, actA, psA, psS, psT, psPT = self.din, self.actA, self.psA, self.psS, self.psT, self.psPT
        has_s = ps_idx == 0
        last_pass = ps_idx == NPASS - 1
        self.ar_reset()
        b1, b2, b3, b4 = [self.ar_f32(TMAX) for _ in range(4)]
        vF, gs, qt, kt, kl = [self.ar_bf(TMAX) for _ in range(5)]
        ecl = self.ar_f32(20)
        klT, vTt = self.ar_bf(20, 128), self.ar_bf(20, 128)
        S = self.ar_f32(128)
        Sall = self.ar_bf(16, 128)
        amT = self.ar_bf(512)
        segm = self.ar_f32(TMAX)
        tri = self.ar_f32(512)
        ss = self.st1
        self.dma([(segm, d["segmask"]), (tri[:32, :], d["trimask"])])

        def vinit(e):
            e.memset(ss, 0.0)
            if has_s:
                e.memset(actA[:, :, TP:TMAX], 0.0)
        self.V(vinit)
        TT = lambda e, o, a, b, op: e.tensor_tensor(out=o, in0=a, in1=b, op=op)
        NCH = TP // 32
        for h in range(32):
            if ps_idx == 0:
                self.V(lambda e: e.memset(S, 0.0))
            else:
                self.dma([(S, self.hst_d[h])])

            def epi(e, i, j, ps):
                last = None
                for (c0, cn) in nt:
                    if i == 0:
                        last = e.activation(out=b1[:, c0:c0 + cn], in_=ps(c0, cn), func=AF.Silu)
                    elif i == 1:
                        last = e.activation(out=b2[:, c0:c0 + cn], in_=ps(c0, cn), func=AF.Sigmoid)
                    elif i == 2:
                        last = e.activation(out=vF[:, c0:c0 + cn], in_=ps(c0, cn), func=AF.Copy)
                    else:
                        last = e.activation(out=gs[:, c0:c0 + cn], in_=ps(c0, cn), func=AF.Sigmoid)
                return last
            self.linear(d["w_oin"], [h, 32 + h, 64 + h, 96 + h], KT, self.xact(self.xb), nt, epi, "scalar")
            self.V(lambda e, h=h: e.tensor_scalar(out=b2[:, :T], in0=b2[:, :T], scalar1=self.hg_oml[:, h:h + 1],
                                                  scalar2=self.hg_lb[:, h:h + 1], op0=ALU.mult, op1=ALU.add))
            self.A(lambda e: e.activation(out=b3[:, :T], in_=b2[:, :T], func=AF.Ln))

            def v2(e):
                e.tensor_scalar(out=b2[:, :T], in0=b2[:, :T], scalar1=-1.0, scalar2=1.0, op0=ALU.mult, op1=ALU.add)
                e.tensor_tensor_scan(out=b4[:, :T], data0=segm[:, :T], data1=b3[:, :T], initial=0.0,
                                     op0=ALU.mult, op1=ALU.add)
            self.V(v2)
            def aexp(e):
                e.activation(out=b3[:, :T], in_=b4[:, :T], func=AF.Exp)
                e.activation(out=b4[:, :T], in_=b4[:, :T], func=AF.Exp, scale=-1.0)
            self.A(aexp)

            def v4(e):
                TT(e, qt[:, :T], b1[:, :T], b3[:, :T], ALU.mult)
                e.tensor_copy(out=ecl[:, 0:NCH], in_=b3[:, 31:TP:32])
                if has_s:
                    e.tensor_copy(out=ecl[:, 16:20], in_=b3[:, TP + 7:TMAX:8])
                TT(e, b1[:, :T], b2[:, :T], b4[:, :T], ALU.mult)
                e.tensor_copy(out=kt[:, :T], in_=b1[:, :T])
                for c in range(NCH):
                    e.tensor_scalar(out=kl[:, 32 * c:32 * c + 32], in0=b1[:, 32 * c:32 * c + 32],
                                    scalar1=ecl[:, c:c + 1], scalar2=None, op0=ALU.mult)
                if has_s:
                    for q in range(NSQ):
                        c0 = scol(q)
                        e.tensor_scalar(out=kl[:, c0:c0 + 4], in0=b1[:, c0:c0 + 4], scalar1=ecl[:, 16 + q:17 + q],
                                        scalar2=None, op0=ALU.mult)
            self.V(v4)
            for src, dst in ((kl, klT), (vF, vTt)):
                for r in range(2):
                    def ptr(e, src=src, r=r):
                        for cc in range(8):
                            c = 8 * r + cc
                            e.transpose(out=psPT[:32, cc * 128:(cc + 1) * 128], in_=src[:, 32 * c:32 * c + 32],
                                        identity=self.identB[:, :])
                    self.P(ptr)
                    self.V(lambda e, dst=dst, r=r: e.tensor_copy(
                        out=dst[:32, 8 * r:8 * r + 8, :], in_=psPT[:32, 0:1024].rearrange("p (a b) -> p a b", a=8)))
                if has_s:
                    def ptr2(e, src=src):
                        for q in range(NSQ):
                            c0 = scol(q)
                            e.transpose(out=psPT[:4, q * 128:(q + 1) * 128], in_=src[:, c0:c0 + 4], identity=self.identB[:, :])
                    self.P(ptr2)
                    self.V(lambda e, dst=dst: e.tensor_copy(
                        out=dst[:4, 16:20, :], in_=psPT[:4, 0:512].rearrange("p (a b) -> p a b", a=4)))

            def pat(e):
                for c in range(NCH):
                    e.matmul(psT[:32, 32 * c:32 * c + 32], lhsT=kt[:, 32 * c:32 * c + 32], rhs=qt[:, 32 * c:32 * c + 32],
                             start=True, stop=True)
                for c in range(NCH):
                    e.matmul(psA[:, 128 * c:128 * c + 128], lhsT=klT[:32, c, :], rhs=vTt[:32, c, :], start=True, stop=True)
            self.P(pat)

            def vst(e):
                TT(e, amT[:32, :], psT[:32, 0:512], tri[:32, :], ALU.mult)
                for c in range(NCH):
                    e.tensor_copy(out=Sall[:, c, :], in_=S)
                    e.scalar_tensor_tensor(out=S, in0=S, scalar=ecl[:, c:c + 1], in1=psA[:, 128 * c:128 * c + 128],
                                           op0=ALU.mult, op1=ALU.add)
            self.V(vst)

            def po(e):
                for c in range(NCH):
                    sl = slice(32 * c, 32 * c + 32)
                    e.matmul(psS[:, sl], lhsT=vTt[:32, c, :], rhs=amT[:32, sl], start=True, stop=False)
                    e.matmul(psS[:, sl], lhsT=Sall[:, c, :], rhs=qt[:, sl], start=False, stop=True)
            self.P(po)
            self.A(lambda e: e.activation(out=b3[:, 0:TP], in_=psS[:, 0:TP], func=AF.Square))

            def vo(e, h=h):
                TT(e, b4[:, 0:TP], psS[:, 0:TP], gs[:, 0:TP], ALU.mult)
                e.tensor_scalar(out=actA[:, h, 0:TP], in0=b4[:, 0:TP], scalar1=self.hg_gn[:, h:h + 1], scalar2=None, op0=ALU.mult)
                TT(e, ss[:, 0:TP], ss[:, 0:TP], b3[:, 0:TP], ALU.add)
            self.V(vo)
            if last_pass:
                self.outdma([(self.dout["p_hgrn"][h], S)])
            else:
                self.dma([(self.hst_d[h], S)])
            if has_s:
                for q in range(NSQ):
                    c0 = scol(q)
                    self.dma([(S, d["c_hg"][q, h])])

                    def p1(e, q=q, c0=c0):
                        e.matmul(psT[:4, 0:4], lhsT=kt[:, c0:c0 + 4], rhs=qt[:, c0:c0 + 4], start=True, stop=True)
                        e.matmul(psA[:, 0:128], lhsT=klT[:4, 16 + q, :], rhs=vTt[:4, 16 + q, :], start=True, stop=True)
                    self.P(p1)

                    def v1(e, q=q):
                        TT(e, amT[:4, 0:4], psT[:4, 0:4], tri[:4, 0:4], ALU.mult)
                        e.tensor_copy(out=Sall[:, 0, :], in_=S)
                        e.scalar_tensor_tensor(out=S, in0=S, scalar=ecl[:, 16 + q:17 + q], in1=psA[:, 0:128],
                                               op0=ALU.mult, op1=ALU.add)
                    self.V(v1)

                    def p2(e, q=q, c0=c0):
                        e.matmul(psS[:, 0:4], lhsT=vTt[:4, 16 + q, :], rhs=amT[:4, 0:4], start=True, stop=False)
                        e.matmul(psS[:, 0:4], lhsT=Sall[:, 0, :], rhs=qt[:, c0:c0 + 4], start=False, stop=True)
                    self.P(p2)
                    self.A(lambda e: e.activation(out=b3[:, 0:4], in_=psS[:, 0:4], func=AF.Square))

                    def v2s(e, h=h, c0=c0):
                        TT(e, b4[:, 0:4], psS[:, 0:4], gs[:, c0:c0 + 4], ALU.mult)
                        e.tensor_scalar(out=actA[:, h, c0:c0 + 4], in0=b4[:, 0:4], scalar1=self.hg_gn[:, h:h + 1],
                                        scalar2=None, op0=ALU.mult)
                        TT(e, ss[:, c0:c0 + 4], ss[:, c0:c0 + 4], b3[:, 0:4], ALU.add)
                    self.V(v2s)
                    self.outdma([(self.dout["s_hgrn"][q, h], S)])
        self.A(lambda e: e.activation(out=qt[:, :T], in_=ss[:, :T], func=AF.Copy))

        def vlo(e):
            e.tensor_copy(out=b1[:, :T], in_=qt[:, :T])
            TT(e, b1[:, :T], ss[:, :T], b1[:, :T], ALU.subtract)
            e.tensor_copy(out=kt[:, :T], in_=b1[:, :T])
        self.V(vlo)

        def pss(e):
            for (c0, cn) in nt:
                e.matmul(psS[:, c0:c0 + cn], lhsT=self.onesB[:, :], rhs=qt[:, c0:c0 + cn], start=True, stop=False)
                e.matmul(psS[:, c0:c0 + cn], lhsT=self.onesB[:, :], rhs=kt[:, c0:c0 + cn], start=False, stop=True)
        self.P(pss)
        self.V(lambda e: e.tensor_scalar(out=ss[:, :T], in0=psS[:, :T], scalar1=1.0 / D, scalar2=1e-6, op0=ALU.mult, op1=ALU.add))
        self.A(lambda e: e.activation(out=ss[:, :T], in_=ss[:, :T], func=AF.Sqrt))
        self.V(lambda e: e.reciprocal(out=ss[:, :T], in_=ss[:, :T]))
        xres = self.xres

        def epi_o(e, i, j, ps):
            last = None
            for (c0, cn) in nt:
                e.tensor_tensor(out=b2[:, c0:c0 + cn], in0=ps(c0, cn), in1=ss[:, c0:c0 + cn], op=ALU.mult)
                last = e.scalar_tensor_tensor(out=xres[:, j, c0:c0 + cn], in0=xres[:, j, c0:c0 + cn], scalar=ALPHA,
                                              in1=b2[:, c0:c0 + cn], op0=ALU.mult, op1=ALU.add)
            return last
        self.linear(d["w_oout"], list(range(KT)), KT, self.xact(actA), nt, epi_o, "vector")

    def mixer(self, l, ps_idx, T, nt):
        if DBG.get('stub'):
            def v(e):
                for j in range(KT):
                    e.tensor_scalar(out=self.xres[:, j, :T], in0=self.xres[:, j, :T], scalar1=ALPHA, scalar2=None, op0=ALU.mult)
            self.V(v)
            return
        if l == 0:
            self.even_mixer(ps_idx, T, nt)
        else:
            self.odd_mixer(ps_idx, T, nt)

    def build(self):
        nc = self.nc
        I, O = self.inp, self.outp
        I("xT", [NPASS, KT, 128, TP]); I("xsT", [KT, 128, SC]); I("memT", [KT, 128, 256])
        I("c_mem_k", [2, NSQ, 256, 512]); I("c_mem_v", [2, NSQ, 256, 512])
        I("lng", [128, 6 * KT]); I("lnb", [128, 6 * KT]); I("identF", [128, 128]); I("onesF", [128, 128])
        I("w_gate", [2, FT, 128, KT, 128]); I("w_up", [2, FT, 128, KT, 128])
        for r, (a, b) in enumerate(FF_ROUNDS):
            I("w_down%d" % r, [2, KT, 128, b - a, 128])
        for n in ("w_mq", "w_mk", "w_mv"):
            I(n, [2, 4, 128, KT, 128])
        I("w_mo", [2, KT, 128, 4, 128])
        I("w_ein", [40, 128, KT, 128]); I("w_glu", [16, 128, 16, 128]); I("w_eout", [KT, 128, KT, 128])
        I("w_oin", [128, 128, KT, 128]); I("w_oout", [KT, 128, KT, 128])
        for n in ("s5_lre", "s5_lim", "s5_ldt"):
            I(n, [128, 64])
        for n in ("s5_bre", "s5_bim", "s5_cre", "s5_cim"):
            I(n, [128, 64 * 16])
        I("s5_dA", [128, 16]); I("s5_h0re", [128, 64 * NSQ]); I("s5_h0im", [128, 64 * NSQ])
        I("rel_bias", [32, 32]); I("ohrev", [32, 128]); I("Jmat", [128, 128]); I("sinkb", [128, 32])
        I("iota1", [128, 512]); I("segmask", [128, TMAX]); I("trimask", [32, 512])
        I("hg_l0", [128, 32]); I("hg_l1", [128, 32]); I("hg_gn", [128, 32])
        I("c_swa_k", [NSQ, 128, 512]); I("c_swa_v", [NSQ, 128, 512]); I("c_hg", [NSQ, 32, 128, 128])
        O("y_p", [NPASS, KT, 128, TP]); O("y_s", [KT, 128, SC])
        O("p_mem_k", [2, 256, 512]); O("p_mem_v", [2, 256, 512])
        O("p_swa_k", [128, 512]); O("p_swa_v", [128, 512]); O("p_s5_re", [64, 128]); O("p_s5_im", [64, 128])
        O("p_hgrn", [32, 128, 128])
        O("s_swa_k", [NSQ, 128, 512]); O("s_swa_v", [NSQ, 128, 512]); O("s_s5_re", [256, 128]); O("s_s5_im", [256, 128])
        O("s_hgrn", [NSQ, 32, 128, 128])
        for nm, shp in DBG.get('extra_outputs', []):
            self.dout[nm] = self.nc.dram_tensor(nm, list(shp), F32, kind="ExternalOutput").ap()

        sb = self.sb
        self.xb = sb("xb", [128, KT, TMAX], BF16)
        self.xres = sb("xres", [128, KT, TMAX], F32)
        self.actA = sb("actA", [128, KT, TMAX], BF16)
        self.wbuf = sb("wbuf", [128, 2, KT, 128], BF16)
        self.mkF = sb("mkF", [128, 4, 256], BF16)
        self.mvT = sb("mvT", [128, 2, 512], BF16)
        self.kh = sb("kh", [128, 4, 128], BF16)
        self.vh = sb("vh", [128, 512], BF16)
        self.lng = sb("lng_sb", [128, 6 * KT], F32)
        self.lnb = sb("lnb_sb", [128, 6 * KT], F32)
        self.identF = sb("identF_sb", [128, 128], F32)
        self.identB = sb("identB_sb", [128, 128], BF16)
        self.onesB = sb("onesB_sb", [128, 128], BF16)
        self.s5_mag = sb("s5_mag", [128, 64], F32); self.s5_thr = sb("s5_thr", [128, 64], F32)
        self.s5_Hre = sb("s5_Hre", [128, 64], F32); self.s5_Him = sb("s5_Him", [128, 64], F32)
        self.s5_fre = sb("s5_fre", [128, NSQ, 64], F32); self.s5_fim = sb("s5_fim", [128, NSQ, 64], F32)
        self.s5_dA = sb("s5_dA_sb", [128, 16], F32)
        self.sinkb = sb("sinkb_sb", [128, 32], F32)
        self.hg_lb = sb("hg_lb", [128, 32], F32); self.hg_oml = sb("hg_oml", [128, 32], F32)
        self.hg_gn = sb("hg_gn_sb", [128, 32], F32)
        self.AR_WORDS = 10240
        self.AR_COMMON = 1600
        self.scr = sb("scr", [128, self.AR_WORDS], F32)
        self.ar_o = 0
        self.st1 = self.ar_f32(TMAX)
        self.at_s = self.ar_f32(512)
        self.at_mx = self.ar_f32(2)
        self.at_sm = self.ar_f32(2)
        self.at_p = self.ar_bf(512)
        self.at_pt = self.ar_bf(4, 128)
        assert self.ar_o <= self.AR_COMMON
        self.psA = nc.alloc_psum_tensor("psA", [128, 2048], F32)
        self.psS = nc.alloc_psum_tensor("psS", [128, 1024], F32)
        self.psT = nc.alloc_psum_tensor("psT", [128, 512], F32)
        self.psPT = nc.alloc_psum_tensor("psPT", [128, 1024], BF16)
        self.psTr = self.psT
        self.psO = self.psS
        self.dsem = self.sem("dsem"); self.osem = self.sem("osem")
        self.svc = self.sem("svc"); self.sac = self.sem("sac"); self.sst = self.sem("sst")
        self.sw = [self.sem("sw0"), self.sem("sw1")]
        self.smm = self.sem("smm"); self.sev = self.sem("sev")
        if DBG.get('program') is not None:
            DBG['program'](self)
        else:
            self.program()
        self.flush()
        return nc

    def program(self):
        nc = self.nc
        d = self.din
        self.dma([(self.lng[:, :], d["lng"]), (self.lnb[:, :], d["lnb"]), (self.identF[:, :], d["identF"])])
        self.dma([(self.identB[:, :], d["identF"]), (self.onesB[:, :], d["onesF"])], eng="gpsimd")
        if not DBG.get('stub'):
            self.s5_setup()
            self.swa_setup()
            self.hg_setup()

        for ps_idx in range(DBG.get('npass', NPASS)):
            T = TMAX if ps_idx == 0 else TP
            nt = [(0, TP)] + ([(TP, SC)] if ps_idx == 0 else [])
            loads = [(self.xres[:, :, 0:TP], d["xT"][ps_idx].rearrange("j p t -> p j t"))]
            if ps_idx == 0:
                loads.append((self.xres[:, :, TP:TMAX], d["xsT"].rearrange("j p t -> p j t")))
            self.dma(loads)
            self.A(lambda e, T=T: e.activation(out=self.xb[:, :, :T], in_=self.xres[:, :, :T], func=AF.Copy))
            for l in range(DBG.get('nlayers', 2)):
                self.mixer(l, ps_idx, T, nt)
                self.layernorm(T, nt, l * 3 + 0)
                self.carve_cross()
                self.mem_kv(l, ps_idx == 0)
                self.cross(T, nt, l, ps_idx)
                self.layernorm(T, nt, l * 3 + 1)
                self.ffn(T, nt, l)
                self.layernorm(T, nt, l * 3 + 2)
            outs = [(self.dout["y_p"][ps_idx].rearrange("j p t -> p j t"), self.xres[:, :, 0:TP])]
            if ps_idx == 0:
                outs.append((self.dout["y_s"].rearrange("j p t -> p j t"), self.xres[:, :, TP:TMAX]))
            self.outdma(outs)


def _tile_w(w):
    K, N = w.shape
    return np.ascontiguousarray(w.reshape(K // 128, 128, N // 128, 128).transpose(2, 1, 0, 3))


def _fm(x):
    T, F = x.shape
    return np.ascontiguousarray(x.T.reshape(F // 128, 128, T))


def _t5_bucket(dist):
    n = np.maximum(dist, 0)
    nf = np.maximum(n, 1).astype(np.float32)
    large = 16 + (np.log(nf / np.float32(16)) / np.float32(math.log(128 / 16)) * np.float32(16)).astype(np.int32)
    large = np.minimum(large, 31)
    return np.where(n < 16, n, large)


def _qperm():
    cols = []
    for c in range(16):
        j, i = c // 4, c % 4
        for h in (8 * j + i, 8 * j + 4 + i):
            cols.extend(range(h * 64, h * 64 + 64))
    return np.array(cols)


def _pl(a):
    s = a.shape
    a = a.reshape((64, 2, 64) + s[2:])
    perm = (1, 2, 0) + tuple(range(3, a.ndim))
    return np.ascontiguousarray(a.transpose(perm)).reshape((128, 64) + s[2:])


_CACHE = {}
PCORES = [0, 1, 4, 5]


def kernel(**inp):
    f32 = np.float32
    g = {k: np.asarray(v) for k, v in inp.items()}
    sh = {}
    sh["w_gate"] = np.stack([_tile_w(g["w_ffn_gate"][l]) for l in range(2)])
    sh["w_up"] = np.stack([_tile_w(g["w_ffn_up"][l]) for l in range(2)])
    for r, (a, b) in enumerate(FF_ROUNDS):
        sh["w_down%d" % r] = np.stack([_tile_w(g["w_ffn_down"][l][a * 128:b * 128]) for l in range(2)])
    for n, s in (("w_mq", "w_mem_q"), ("w_mk", "w_mem_k"), ("w_mv", "w_mem_v"), ("w_mo", "w_mem_o")):
        sh[n] = np.stack([_tile_w(g[s][l]) for l in range(2)])
    qp = _qperm()
    win = g["w_even_in"][0]
    cols = np.concatenate([np.arange(2048), 2048 + qp, np.arange(4096, 5120)])
    sh["w_ein"] = _tile_w(win[:, cols])
    sh["w_glu"] = _tile_w(g["s5_w_glu"][0])
    rows = np.concatenate([np.arange(2048), 2048 + qp])
    sh["w_eout"] = _tile_w(g["w_even_out"][0][rows])
    sh["w_oin"] = _tile_w(g["w_odd_in"][0])
    sh["w_oout"] = _tile_w(g["w_odd_out"][0])
    sh["lng"] = np.ascontiguousarray(g["ln_g"].reshape(6, KT, 128).transpose(2, 0, 1).reshape(128, 6 * KT))
    sh["lnb"] = np.ascontiguousarray(g["ln_b"].reshape(6, KT, 128).transpose(2, 0, 1).reshape(128, 6 * KT))
    sh["identF"] = np.eye(128, dtype=f32)
    sh["onesF"] = np.ones((128, 128), f32)
    sh["s5_lre"] = _pl(g["s5_lam_re"][0]); sh["s5_lim"] = _pl(g["s5_lam_im"][0])
    sh["s5_ldt"] = _pl(np.broadcast_to(g["s5_log_dt"][0][:, None], (128, 64)).copy())
    sh["s5_bre"] = _pl(g["s5_b_re"][0]).reshape(128, 1024); sh["s5_bim"] = _pl(g["s5_b_im"][0]).reshape(128, 1024)
    sh["s5_cre"] = _pl(np.ascontiguousarray(g["s5_c_re"][0].transpose(0, 2, 1))).reshape(128, 1024)
    sh["s5_cim"] = _pl(np.ascontiguousarray(g["s5_c_im"][0].transpose(0, 2, 1))).reshape(128, 1024)
    sh["s5_dA"] = np.ascontiguousarray(g["s5_d"][0].reshape(16, 128).T)
    sh["rel_bias"] = np.ascontiguousarray(g["rel_bias"])
    bk = _t5_bucket(127 - np.arange(128))
    oh = np.zeros((32, 128), f32); oh[bk, np.arange(128)] = 1.0
    sh["ohrev"] = oh
    sh["Jmat"] = np.ascontiguousarray(np.eye(128, dtype=f32)[::-1])
    sh["sinkb"] = np.ascontiguousarray(np.broadcast_to(g["swa_sinks"][0][None, :], (128, 32))).astype(f32)
    sh["iota1"] = np.ascontiguousarray(np.broadcast_to(np.arange(1, 513, dtype=f32)[None, :], (128, 512)))
    seg = np.ones(TMAX, f32); seg[0:TP:32] = 0.0; seg[TP:] = 0.0
    for q in range(NSQ):
        seg[TP + 8 * q + 5:TP + 8 * q + 8] = 1.0
    sh["segmask"] = np.ascontiguousarray(np.broadcast_to(seg[None, :], (128, TMAX)))
    tri = np.triu(np.ones((32, 32), f32))
    sh["trimask"] = np.ascontiguousarray(np.tile(tri, (1, 16)))
    sh["hg_l0"] = np.ascontiguousarray(g["hg_lb_logits"][0].reshape(32, 128).T)
    sh["hg_l1"] = np.ascontiguousarray(g["hg_lb_logits"][1].reshape(32, 128).T)
    sh["hg_gn"] = np.ascontiguousarray(g["hg_norm_g"][0].reshape(32, 128).T)
    in_maps = []
    for c in range(8):
        m = dict(sh)
        if c in PCORES:
            b = PCORES.index(c)
            m["xT"] = np.stack([_fm(g["x_prompt"][b, p * TP:(p + 1) * TP]) for p in range(NPASS)])
            m["memT"] = _fm(g["mem_prompt"][b])
        else:
            m["xT"] = np.zeros((NPASS, KT, 128, TP), f32)
            m["memT"] = np.zeros((KT, 128, 256), f32)
        xs = np.zeros((SC, D), f32)
        for q in range(NSQ):
            xs[8 * q + 4:8 * q + 8] = g["x_sample"][c * NSQ + q]
        m["xsT"] = _fm(xs)
        sl = slice(c * NSQ, (c + 1) * NSQ)
        m["c_mem_k"] = np.ascontiguousarray(g["cache_mem_k"][:, sl].reshape(2, NSQ, 256, 512))
        m["c_mem_v"] = np.ascontiguousarray(g["cache_mem_v"][:, sl].reshape(2, NSQ, 256, 512))
        m["c_swa_k"] = np.ascontiguousarray(g["cache_swa_k"][0, sl].reshape(NSQ, 128, 512))
        m["c_swa_v"] = np.ascontiguousarray(g["cache_swa_v"][0, sl].reshape(NSQ, 128, 512))
        m["c_hg"] = np.ascontiguousarray(g["state_hgrn"][0, sl])
        for nm, src in (("s5_h0re", "state_s5_re"), ("s5_h0im", "state_s5_im")):
            a = g[src][0, sl]
            a = a.reshape(NSQ, 64, 2, 64).transpose(2, 3, 1, 0)
            m[nm] = np.ascontiguousarray(a).reshape(128, 64 * NSQ)
        in_maps.append(m)
    if "nc" not in _CACHE:
        _CACHE["nc"] = Builder().build()
    res = run_bass_kernel_spmd(_CACHE["nc"], in_maps, core_ids=list(range(8))).results
    y_p = np.stack([np.concatenate([res[c]["y_p"][p].reshape(D, TP).T for p in range(NPASS)], 0) for c in PCORES])
    y_s = np.zeros((32, 4, D), f32)
    for c in range(8):
        ys = res[c]["y_s"].reshape(D, SC).T
        for q in range(NSQ):
            y_s[c * NSQ + q] = ys[8 * q + 4:8 * q + 8]
    P4 = lambda nm: np.stack([res[c][nm] for c in PCORES])
    S8 = lambda nm: np.concatenate([res[c][nm] for c in range(8)], 0)
    p_mem_k = P4("p_mem_k").transpose(1, 0, 2, 3).reshape(2, 4, 256, 4, 128)
    p_mem_v = P4("p_mem_v").transpose(1, 0, 2, 3).reshape(2, 4, 256, 4, 128)
    p_swa_k = P4("p_swa_k").reshape(1, 4, 128, 8, 64)
    p_swa_v = P4("p_swa_v").reshape(1, 4, 128, 8, 64)
    p_s5_re = P4("p_s5_re").reshape(1, 4, 128, 64)
    p_s5_im = P4("p_s5_im").reshape(1, 4, 128, 64)
    p_hg = P4("p_hgrn").reshape(1, 4, 32, 128, 128)
    s_swa_k = S8("s_swa_k").reshape(1, 32, 128, 8, 64)
    s_swa_v = S8("s_swa_v").reshape(1, 32, 128, 8, 64)
    s_s5_re = S8("s_s5_re").reshape(1, 32, 128, 64)
    s_s5_im = S8("s_s5_im").reshape(1, 32, 128, 64)
    s_hg = S8("s_hgrn").reshape(1, 32, 32, 128, 128)
    outs = (y_p, y_s, p_mem_k, p_mem_v, p_swa_k, p_swa_v, p_s5_re, p_s5_im, p_hg,
            s_swa_k, s_swa_v, s_s5_re, s_s5_im, s_hg)
    return tuple(np.ascontiguousarray(o, dtype=f32) for o in outs)
```

```python
import math
import numpy as np
import concourse.bass as bass
import concourse.mybir as mybir
from concourse.bass_utils import run_bass_kernel_spmd

F32 = mybir.dt.float32
BF16 = mybir.dt.bfloat16
AF = mybir.ActivationFunctionType
ALU = mybir.AluOpType
AX = mybir.AxisListType

D = 4096
KT = 32
TP = 512
NPASS = 4
NSQ = 4
SC = 32
TMAX = TP + SC
DFF = 11008
FT = DFF // 128
ALPHA = 4.0 ** 0.25
MAGIC = 12582912.0
TWO_PI = 2.0 * math.pi
FF_ROUNDS = [(0, 32), (32, 64), (64, 86)]
DBG = {}


def scol(q, t=0):
    return TP + 8 * q + 4 + t


class Sem:
    def __init__(self, nc, name):
        self.h = nc.alloc_semaphore(name)
        self.n = 0


class Chain:
    def __init__(self, e, sem):
        self.e, self.sem, self.pending = e, sem, None

    def __getattr__(self, name):
        f = getattr(self.e, name)

        def call(*a, **k):
            if self.pending is not None:
                self.pending.then_inc(self.sem.h, 1)
                self.sem.n += 1
                self.e.wait_ge(self.sem.h, self.sem.n)
            ins = f(*a, **k)
            self.pending = ins
            return ins
        return call

    def finish(self, sst):
        self.pending.then_inc(sst.h, 1)


class Last:
    def __init__(self, e):
        self.e, self.last = e, None

    def __getattr__(self, name):
        f = getattr(self.e, name)

        def call(*a, **k):
            ins = f(*a, **k)
            self.last = ins
            return ins
        return call


class Builder:
    def __init__(self):
        self.nc = bass.Bass("TRN2", target_bir_lowering=False)
        self.din = {}
        self.dout = {}
        self.out_sem = None
        self.nsem = 0
        self.stages = []

    def inp(self, name, shape, dt=F32):
        if DBG.get('inputs') is not None and name not in DBG['inputs']:
            return None
        self.din[name] = self.nc.dram_tensor(name, list(shape), dt, kind="ExternalInput").ap()
        return self.din[name]

    def outp(self, name, shape):
        if DBG.get('outputs') is not None and name not in DBG['outputs']:
            return None
        self.dout[name] = self.nc.dram_tensor(name, list(shape), F32, kind="ExternalOutput").ap()
        return self.dout[name]

    def sb(self, name, shape, dt):
        return self.nc.alloc_sbuf_tensor(name, list(shape), dt)

    def sem(self, name):
        self.nsem += 1
        return Sem(self.nc, name)

    def blk(self, **fns):
        with self.nc.Block() as b:
            for k, fn in fns.items():
                getattr(b, k)(fn)

    def _rec(self, eng, fn, kind):
        self.stages.append((eng, fn, kind))

    def V(self, fn):
        self._rec("vector", fn, "chain")

    def A(self, fn):
        self._rec("scalar", fn, "chain")

    def P(self, fn):
        self._rec("tensor", fn, "last")

    def _dma(self, sem, pairs, eng, kw):
        sem.n += 16 * len(pairs)
        tgt = sem.n

        def f(e):
            for o, i in pairs:
                e.dma_start(out=o, in_=i, **kw).then_inc(sem.h, 16)
            e.wait_ge(sem.h, tgt)
        self._rec(eng, f, "dma")

    def dma(self, pairs, eng="sync", **kw):
        self._dma(self.dsem, pairs, eng, kw)

    def outdma(self, pairs):
        self._dma(self.osem, pairs, "sync", {})

    def flush(self):
        if not self.stages:
            return
        st, self.stages = self.stages, []
        base = self.sst.n
        by_eng = {}
        for k, (eng, fn, kind) in enumerate(st):
            by_eng.setdefault(eng, []).append((k, fn, kind))
        fns = {}
        for eng, lst in by_eng.items():
            def run(e, lst=lst, eng=eng):
                for (k, fn, kind) in lst:
                    if k > 0:
                        e.wait_ge(self.sst.h, base + k)
                    if kind == "chain":
                        c = Chain(e, self.svc if eng == "vector" else self.sac)
                        fn(c)
                        c.finish(self.sst)
                    elif kind == "last":
                        c = Last(e)
                        fn(c)
                        c.last.then_inc(self.sst.h, 1)
                    else:
                        fn(e)
                        e.sem_inc(self.sst.h, 1)
            fns[eng] = run
        self.blk(**fns)
        self.sst.n += len(st)

    def linear(self, w, chunks, ktw, act, ntiles, epi, epi_eng="scalar"):
        self.flush()
        sw, smm, sev = self.sw, self.smm, self.sev
        bw = [sw[0].n, sw[1].n]
        bmm, bev = smm.n, sev.n
        n = len(chunks)
        psA, wbuf = self.psA, self.wbuf

        def g(e):
            for i, j in enumerate(chunks):
                if i >= 2:
                    e.wait_ge(smm.h, bmm + i - 1)
                e.dma_start(out=wbuf[:, i % 2, :ktw, :], in_=w[j]).then_inc(sw[i % 2].h, 16)

        def pe(e):
            for i, j in enumerate(chunks):
                e.wait_ge(sw[i % 2].h, bw[i % 2] + 16 * (i // 2 + 1))
                if i >= 2:
                    e.wait_ge(sev.h, bev + i - 1)
                ins = None
                o = (i % 2) * 1024
                for (c0, cn) in ntiles:
                    for kt in range(ktw):
                        ins = e.matmul(psA[:, o + c0:o + c0 + cn], lhsT=wbuf[:, i % 2, kt, :],
                                       rhs=act(kt, c0, cn), start=(kt == 0), stop=(kt == ktw - 1))
                ins.then_inc(smm.h, 1)

        def ev(e):
            for i, j in enumerate(chunks):
                e.wait_ge(smm.h, bmm + i + 1)
                o = (i % 2) * 1024
                last = epi(e, i, j, lambda c0, cn, o=o: psA[:, o + c0:o + c0 + cn])
                last.then_inc(sev.h, 1)

        self.blk(**{"gpsimd": g, "tensor": pe, epi_eng: ev})
        sw[0].n += 16 * ((n + 1) // 2)
        sw[1].n += 16 * (n // 2)
        smm.n += n
        sev.n += n

    def epi_copy(self, dst, nt, func=AF.Copy, off=0):
        def epi(e, i, j, ps):
            last = None
            for (c0, cn) in nt:
                last = e.activation(out=dst[:, off + i, c0:c0 + cn], in_=ps(c0, cn), func=func)
            return last
        return epi

    def epi_res(self, first, nt):
        xres = self.xres

        def epi(e, i, j, ps):
            last = None
            for (c0, cn) in nt:
                if first:
                    last = e.scalar_tensor_tensor(out=xres[:, j, c0:c0 + cn], in0=xres[:, j, c0:c0 + cn],
                                                  scalar=ALPHA, in1=ps(c0, cn), op0=ALU.mult, op1=ALU.add)
                else:
                    last = e.tensor_tensor(out=xres[:, j, c0:c0 + cn], in0=xres[:, j, c0:c0 + cn],
                                           in1=ps(c0, cn), op=ALU.add)
            return last
        return epi

    def xact(self, buf):
        return lambda kt, c0, cn: buf[:, kt, c0:c0 + cn]

    def layernorm(self, T, nt, idx):
        xres, xb, actA, psS, st1 = self.xres, self.xb, self.actA, self.psS, self.st1

        def stat(e):
            for (c0, cn) in nt:
                for j in range(KT):
                    e.matmul(psS[:, c0:c0 + cn], lhsT=self.onesB[:, :], rhs=actA[:, j, c0:c0 + cn],
                             start=(j == 0), stop=(j == KT - 1))
        self.A(lambda e: e.activation(out=actA[:, :, :T], in_=xres[:, :, :T], func=AF.Copy))
        self.P(stat)

        def v1(e):
            e.tensor_scalar(out=st1[:, :T], in0=psS[:, :T], scalar1=1.0 / D, scalar2=None, op0=ALU.mult)
            e.tensor_tensor(out=xres[:, :, :T], in0=xres[:, :, :T],
                            in1=st1[:, :T].unsqueeze(1).to_broadcast([128, KT, T]), op=ALU.subtract)
        self.V(v1)
        self.A(lambda e: e.activation(out=actA[:, :, :T], in_=xres[:, :, :T], func=AF.Square))
        self.P(stat)
        self.V(lambda e: e.tensor_scalar(out=st1[:, :T], in0=psS[:, :T], scalar1=1.0 / D, scalar2=1e-5,
                                         op0=ALU.mult, op1=ALU.add))
        self.A(lambda e: e.activation(out=st1[:, :T], in_=st1[:, :T], func=AF.Sqrt))

        def v2(e):
            e.reciprocal(out=st1[:, :T], in_=st1[:, :T])
            e.tensor_tensor(out=xres[:, :, :T], in0=xres[:, :, :T],
                            in1=st1[:, :T].unsqueeze(1).to_broadcast([128, KT, T]), op=ALU.mult)
            gsl = self.lng[:, idx * KT:(idx + 1) * KT].unsqueeze(2).to_broadcast([128, KT, T])
            bsl = self.lnb[:, idx * KT:(idx + 1) * KT].unsqueeze(2).to_broadcast([128, KT, T])
            e.tensor_tensor(out=xres[:, :, :T], in0=xres[:, :, :T], in1=gsl, op=ALU.mult)
            e.tensor_tensor(out=xres[:, :, :T], in0=xres[:, :, :T], in1=bsl, op=ALU.add)
        self.V(v2)
        self.A(lambda e: e.activation(out=xb[:, :, :T], in_=xres[:, :, :T], func=AF.Copy))

    def ffn(self, T, nt, l):
        actA = self.actA
        for r, (a, b) in enumerate(FF_ROUNDS):
            self.linear(self.din["w_gate"][l], list(range(a, b)), KT, self.xact(self.xb), nt,
                        self.epi_copy(actA, nt, AF.Silu), "scalar")

            def epi_u(e, i, j, ps):
                last = None
                for (c0, cn) in nt:
                    last = e.tensor_tensor(out=actA[:, i, c0:c0 + cn], in0=actA[:, i, c0:c0 + cn],
                                           in1=ps(c0, cn), op=ALU.mult)
                return last
            self.linear(self.din["w_up"][l], list(range(a, b)), KT, self.xact(self.xb), nt, epi_u, "vector")
            self.linear(self.din["w_down%d" % r][l], list(range(KT)), b - a, self.xact(actA), nt,
                        self.epi_res(r == 0, nt), "vector")

    def attn(self, M, q_ap, kf_list, vt_list, out_ps_rows, out_ap, scale, bias_ap=None, sink_ap=None, extra_mask=None):
        psT, psPT, psO = self.psT, self.psPT, self.psO
        sb, pn, ptb, mx, sm = self.at_s, self.at_p, self.at_pt, self.at_mx, self.at_sm
        nks = [k.shape[-1] for k in kf_list]
        NK = sum(nks)
        offs = [sum(nks[:i]) for i in range(len(nks))]

        def p1(e):
            for k, o in zip(kf_list, offs):
                e.matmul(psT[:M, o:o + k.shape[-1]], lhsT=q_ap, rhs=k, start=True, stop=True)
        self.P(p1)

        def v1(e):
            if bias_ap is not None:
                e.scalar_tensor_tensor(out=sb[:M, :NK], in0=psT[:M, :NK], scalar=scale, in1=bias_ap,
                                       op0=ALU.mult, op1=ALU.add)
            else:
                e.tensor_scalar(out=sb[:M, :NK], in0=psT[:M, :NK], scalar1=scale, scalar2=None, op0=ALU.mult)
            if extra_mask is not None:
                a, b = extra_mask
                e.tensor_scalar(out=sb[:M, a:b], in0=sb[:M, a:b], scalar1=-1e30, scalar2=None, op0=ALU.add)
            e.tensor_reduce(out=mx[:M, 0:1], in_=sb[:M, :NK], axis=AX.X, op=ALU.max)
            if sink_ap is not None:
                e.tensor_tensor(out=mx[:M, 0:1], in0=mx[:M, 0:1], in1=sink_ap, op=ALU.max)
            e.tensor_scalar(out=mx[:M, 1:2], in0=mx[:M, 0:1], scalar1=-1.0, scalar2=None, op0=ALU.mult)
        self.V(v1)

        def a1(e):
            e.activation(out=sb[:M, :NK], in_=sb[:M, :NK], func=AF.Exp, bias=mx[:M, 1:2], scale=1.0)
            if sink_ap is not None:
                e.activation(out=sm[:M, 1:2], in_=sink_ap, func=AF.Exp, bias=mx[:M, 1:2], scale=1.0)
        self.A(a1)

        def v2(e):
            e.tensor_reduce(out=sm[:M, 0:1], in_=sb[:M, :NK], axis=AX.X, op=ALU.add)
            if sink_ap is not None:
                e.tensor_tensor(out=sm[:M, 0:1], in0=sm[:M, 0:1], in1=sm[:M, 1:2], op=ALU.add)
            e.reciprocal(out=sm[:M, 0:1], in_=sm[:M, 0:1])
            e.tensor_scalar(out=pn[:M, :NK], in0=sb[:M, :NK], scalar1=sm[:M, 0:1], scalar2=None, op0=ALU.mult)
        self.V(v2)

        def p2(e):
            for i, (nk, o) in enumerate(zip(nks, offs)):
                e.transpose(out=psPT[:nk, i * 128:i * 128 + M], in_=pn[:M, o:o + nk], identity=self.identB[:M, :M])
        self.P(p2)

        def v3(e):
            for i, nk in enumerate(nks):
                e.tensor_copy(out=ptb[:nk, i, :M], in_=psPT[:nk, i * 128:i * 128 + M])
        self.V(v3)
        r0, dh = out_ps_rows

        def p3(e):
            for i, (v, nk) in enumerate(zip(vt_list, nks)):
                e.matmul(psO[r0:r0 + dh, :M], lhsT=v, rhs=ptb[:nk, i, :M], start=(i == 0), stop=(i == len(nks) - 1))
        self.P(p3)
        self.A(lambda e: e.activation(out=out_ap, in_=psO[r0:r0 + dh, :M], func=AF.Copy))

    def tr32(self, e, src, rows, cols, slot=0):
        e.transpose(out=self.psTr[:cols, slot * 128:slot * 128 + rows], in_=src, identity=self.identF[:rows, :rows])

    def mem_kv(self, l, write_out=True):
        nt = [(0, 256)]
        memT = self.actA
        self.dma([(memT[:, :, 0:256], self.din["memT"].rearrange("j p t -> p j t"))], eng="gpsimd")
        f32 = self.f32tmp

        def epi_k(e, i, j, ps):
            e.activation(out=f32[:, i, 0:256], in_=ps(0, 256), func=AF.Copy)
            return e.activation(out=self.mkF[:, i, :], in_=ps(0, 256), func=AF.Copy)
        self.linear(self.din["w_mk"][l], list(range(4)), KT, self.xact(memT), nt, epi_k, "scalar")

        def epi_v(e, i, j, ps):
            return e.activation(out=f32[:, 4 + i, 0:256], in_=ps(0, 256), func=AF.Copy)
        self.linear(self.din["w_mv"][l], list(range(4)), KT, self.xact(memT), nt, epi_v, "scalar")
        for which, outname in ((0, "p_mem_k"), (1, "p_mem_v")):
            for blk in range(2):
                def pt(e, which=which, blk=blk):
                    for hd in range(4):
                        self.tr32(e, f32[:, which * 4 + hd, blk * 128:(blk + 1) * 128], 128, 128, slot=hd)
                self.P(pt)

                def cp(e, which=which, blk=blk):
                    e.tensor_copy(out=self.tm32[:, 0:512], in_=self.psTr[:, 0:512])
                    if which == 1:
                        e.tensor_copy(out=self.mvT[:, blk, :], in_=self.psTr[:, 0:512])
                self.V(cp)
                if write_out:
                    self.outdma([(self.dout[outname][l, blk * 128:(blk + 1) * 128, :], self.tm32[:, 0:512])])

    def cross(self, T, nt, l, ps_idx):
        qF = self.actB
        oF = self.actA
        sc = 128 ** -0.5
        self.linear(self.din["w_mq"][l], list(range(4)), KT, self.xact(self.xb), nt, self.epi_copy(qF, nt), "scalar")
        if ps_idx == 0:
            self.V(lambda e: e.memset(oF[:, 0:4, TP:TMAX], 0.0))
        for tb in range(TP // 128):
            for hd in range(4):
                self.attn2(qF, oF, l, hd, tb, sc)
        if ps_idx == 0:
            for q in range(NSQ):
                self.cross_sample(qF, oF, l, q, sc)
        self.linear(self.din["w_mo"][l], list(range(KT)), 4, self.xact(oF), nt, self.epi_res(True, nt), "vector")

    def attn2(self, qF, oF, l, hd, tb, sc):
        kf = [self.mkF[:, hd, 0:128], self.mkF[:, hd, 128:256]]
        vt = [self.mvT[:, b, hd * 128:(hd + 1) * 128] for b in range(2)]
        self.attn(128, qF[:, hd, tb * 128:(tb + 1) * 128], kf, vt, (0, 128), oF[:, hd, tb * 128:(tb + 1) * 128], sc)

    def cross_sample(self, qF, oF, l, q, sc):
        ck = self.din["c_mem_k"][l, q].rearrange("(b t) f -> t b f", t=128)
        cv = self.din["c_mem_v"][l, q].rearrange("(b t) f -> t b f", t=128)
        self.dma([(self.ck32[:, :, :], ck), (self.cv32[:, :, :], cv)])
        for b in range(2):
            def pt(e, b=b):
                for hd in range(4):
                    self.tr32(e, self.ck32[:, b, hd * 128:(hd + 1) * 128], 128, 128, slot=hd)
            self.P(pt)
            self.V(lambda e, b=b: e.tensor_copy(out=self.ckF[:, :, b * 128:(b + 1) * 128],
                                               in_=self.psTr[:, 0:512].rearrange("p (h t) -> p h t", h=4)))
        self.V(lambda e: e.tensor_copy(out=self.cvT[:, :, :], in_=self.cv32[:, :, :]))
        c0 = scol(q)
        for hd in range(4):
            kf = [self.ckF[:, hd, 0:128], self.ckF[:, hd, 128:256]]
            vt = [self.cvT[:, b, hd * 128:(hd + 1) * 128] for b in range(2)]
            self.attn(4, qF[:, hd, c0:c0 + 4], kf, vt, (0, 128), oF[:, hd, c0:c0 + 4], sc)


    def ar_reset(self, keep_common=True):
        self.ar_o = self.AR_COMMON if keep_common else 0

    def ar_f32(self, *shape):
        n = int(np.prod(shape))
        v = self.scr[:, self.ar_o:self.ar_o + n]
        self.ar_o += n
        assert self.ar_o <= self.AR_WORDS, ("arena overflow", self.ar_o)
        return self._shape(v, shape)

    def ar_bf(self, *shape):
        n = int(np.prod(shape))
        w = (n + 1) // 2
        v = self.scr[:, self.ar_o:self.ar_o + w].bitcast(BF16)[:, :n]
        self.ar_o += w
        assert self.ar_o <= self.AR_WORDS, ("arena overflow", self.ar_o)
        return self._shape(v, shape)

    @staticmethod
    def _shape(v, shape):
        if len(shape) == 1:
            return v
        if len(shape) == 2:
            return v.rearrange("p (a b) -> p a b", a=shape[0])
        return v.rearrange("p (a b c) -> p a b c", a=shape[0], b=shape[1])

    def carve_cross(self):
        self.ar_reset()
        self.f32tmp = self.ar_f32(8, 256)
        self.ck32 = self.f32tmp[:, 0:4, :].rearrange("p (a b) c -> p a (b c)", a=2)
        self.cv32 = self.f32tmp[:, 4:8, :].rearrange("p (a b) c -> p a (b c)", a=2)
        self.tm32 = self.ar_f32(512)
        self.ckF = self.ar_bf(4, 256)
        self.cvT = self.ar_bf(2, 512)
        self.actB = self.ar_bf(4, TMAX)

    def rred(self, e, x, tmp):
        e.tensor_scalar(out=tmp, in0=x, scalar1=1.0 / TWO_PI, scalar2=MAGIC, op0=ALU.mult, op1=ALU.add)
        e.tensor_scalar(out=tmp, in0=tmp, scalar1=MAGIC, scalar2=-TWO_PI, op0=ALU.subtract, op1=ALU.mult)
        e.tensor_tensor(out=x, in0=x, in1=tmp, op=ALU.add)
        e.tensor_scalar(out=x, in0=x, scalar1=3.141592, scalar2=-3.141592, op0=ALU.min, op1=ALU.max)

    def s5_setup(self):
        d = self.din
        nc = self.nc
        self.cst_d = nc.dram_tensor("cst_d", [16, 128, 1024], BF16).ap()
        self.bst_d = nc.dram_tensor("bst_d", [16, 128, 1024], BF16).ap()
        self.ar_reset(False)
        A = self.ar_f32
        lre, lim, ldt = A(64), A(64), A(64)
        bre, bim, cre, cim = A(64, 16), A(64, 16), A(64, 16), A(64, 16)
        Bre, Bim = A(64, 16), A(64, 16)
        lr, dt_, t1, th, thc, sn, cs, are, aim, den, am1, fr, fi, nfi = [A(64) for _ in range(14)]
        self.dma([(lre, d["s5_lre"]), (lim, d["s5_lim"]), (ldt, d["s5_ldt"]), (bre, d["s5_bre"]),
                  (bim, d["s5_bim"]), (cre, d["s5_cre"]), (cim, d["s5_cim"]), (self.s5_dA[:, :], d["s5_dA"])])
        mag, thr = self.s5_mag[:, :], self.s5_thr[:, :]
        self.V(lambda e: e.tensor_scalar(out=lr, in0=lre, scalar1=-1e-4, scalar2=None, op0=ALU.min))
        self.A(lambda e: e.activation(out=dt_, in_=ldt, func=AF.Exp))
        self.V(lambda e: e.tensor_tensor(out=t1, in0=lr, in1=dt_, op=ALU.mult))
        self.A(lambda e: e.activation(out=mag, in_=t1, func=AF.Exp))

        def v1(e):
            e.tensor_tensor(out=thr, in0=lim, in1=dt_, op=ALU.mult)
            self.rred(e, thr, t1)
            e.tensor_scalar(out=thc, in0=thr, scalar1=math.pi / 2, scalar2=None, op0=ALU.add)
            self.rred(e, thc, t1)
        self.V(v1)

        def a1(e):
            e.activation(out=sn, in_=thr, func=AF.Sin)
            e.activation(out=cs, in_=thc, func=AF.Sin)
        self.A(a1)
        TT = lambda e, o, a, b, op: e.tensor_tensor(out=o, in0=a, in1=b, op=op)

        def v2(e):
            TT(e, are, mag, cs, ALU.mult); TT(e, aim, mag, sn, ALU.mult)
            TT(e, den, lr, lr, ALU.mult); TT(e, t1, lim, lim, ALU.mult); TT(e, den, den, t1, ALU.add)
            e.reciprocal(out=den, in_=den)
            e.tensor_scalar(out=am1, in0=are, scalar1=-1.0, scalar2=None, op0=ALU.add)
            TT(e, fr, am1, lr, ALU.mult); TT(e, t1, aim, lim, ALU.mult); TT(e, fr, fr, t1, ALU.add); TT(e, fr, fr, den, ALU.mult)
            TT(e, fi, aim, lr, ALU.mult); TT(e, t1, am1, lim, ALU.mult); TT(e, fi, fi, t1, ALU.subtract); TT(e, fi, fi, den, ALU.mult)
            e.tensor_scalar(out=nfi, in0=fi, scalar1=-1.0, scalar2=None, op0=ALU.mult)
            for pr in range(64):
                e.tensor_scalar(out=Bre[:, pr, :], in0=bre[:, pr, :], scalar1=fr[:, pr:pr + 1], scalar2=None, op0=ALU.mult)
                e.scalar_tensor_tensor(out=Bre[:, pr, :], in0=bim[:, pr, :], scalar=nfi[:, pr:pr + 1], in1=Bre[:, pr, :],
                                       op0=ALU.mult, op1=ALU.add)
                e.tensor_scalar(out=Bim[:, pr, :], in0=bim[:, pr, :], scalar1=fr[:, pr:pr + 1], scalar2=None, op0=ALU.mult)
                e.scalar_tensor_tensor(out=Bim[:, pr, :], in0=bre[:, pr, :], scalar=fi[:, pr:pr + 1], in1=Bim[:, pr, :],
                                       op0=ALU.mult, op1=ALU.add)
            e.tensor_scalar(out=cim, in0=cim, scalar1=-1.0, scalar2=None, op0=ALU.mult)
        self.V(v2)
        Ct = self.ar_bf(4, 2, 128)
        Bt = self.ar_bf(4, 2, 128)
        Zp = A(4, 128)
        for ft in range(16):
            def vc(e, ft=ft):
                e.memset(Ct, 0.0)
                for k in range(4):
                    for ee in range(2):
                        r = slice(ee * 64, ee * 64 + 64)
                        c0 = (2 * k + ee) * 16
                        e.tensor_copy(out=Ct[r, k, 0, c0:c0 + 16], in_=cre[r, 4 * ft + k, :])
                        e.tensor_copy(out=Ct[r, k, 1, c0:c0 + 16], in_=cim[r, 4 * ft + k, :])
            self.V(vc)
            for c, Bc in enumerate((Bre, Bim)):
                def vz(e, ft=ft, Bc=Bc):
                    e.memset(Zp, 0.0)
                    for k in range(4):
                        for ee in range(2):
                            r = slice(ee * 64, ee * 64 + 64)
                            c0 = (2 * k + ee) * 16
                            e.tensor_copy(out=Zp[r, k, c0:c0 + 16], in_=Bc[r, 4 * ft + k, :])
                self.V(vz)

                def pz(e):
                    for k in range(4):
                        self.tr32(e, Zp[:, k, :], 128, 128, slot=k)
                self.P(pz)
                self.V(lambda e, c=c: e.tensor_copy(out=Bt[:, :, c, :],
                                                    in_=self.psTr[:, 0:512].rearrange("p (k m) -> p k m", k=4)))
            self.dma([(self.cst_d[ft], Ct.rearrange("p a b c -> p (a b c)")),
                      (self.bst_d[ft], Bt.rearrange("p a b c -> p (a b c)"))])

    def s5_pass(self, ps_idx, T, nt):
        d = self.din
        has_s = ps_idx == 0
        self.ar_reset()
        Bt = self.ar_bf(4, 2, 128)
        Ct = self.ar_bf(4, 2, 128)
        iota1 = self.ar_f32(512)
        Ctab, Stab = self.ar_f32(512), self.ar_f32(512)
        Cs, Ss = self.ar_f32(32), self.ar_f32(32)
        Wr, Wi, Gr, Gi, tmp = [self.ar_f32(TMAX) for _ in range(5)]
        Hbf = self.ar_bf(4, 2, TMAX)
        h0re, h0im = self.ar_f32(64, NSQ), self.ar_f32(64, NSQ)
        loads = [(iota1, d["iota1"])]
        if has_s:
            loads += [(h0re, d["s5_h0re"]), (h0im, d["s5_h0im"])]
        self.dma(loads)
        mag, thr, Hpre, Hpim = self.s5_mag, self.s5_thr, self.s5_Hre, self.s5_Him
        psA = self.psA
        TT = lambda e, o, a, b, op: e.tensor_tensor(out=o, in0=a, in1=b, op=op)

        def vinit(e):
            e.memset(Gr, 0.0); e.memset(Gi, 0.0)
            if ps_idx == 0:
                e.memset(Hpre[:, :], 0.0); e.memset(Hpim[:, :], 0.0)
        self.V(vinit)
        for ft in range(16):
            self.dma([(Bt.rearrange("p a b c -> p (a b c)"), self.bst_d[ft]),
                      (Ct.rearrange("p a b c -> p (a b c)"), self.cst_d[ft])])
            u = self.actA[:, 16 + ft, :]
            for k in range(4):
                pr = 4 * ft + k

                def pv(e, k=k, u=u):
                    for c in range(2):
                        for (c0, cn) in nt:
                            e.matmul(psA[:, c * 1024 + c0:c * 1024 + c0 + cn], lhsT=Bt[:, k, c, :], rhs=u[:, c0:c0 + cn],
                                     start=True, stop=True)
                self.P(pv)

                def vph(e, pr=pr):
                    e.tensor_scalar(out=Wr[:, 0:512], in0=iota1, scalar1=thr[:, pr:pr + 1], scalar2=None, op0=ALU.mult)
                    self.rred(e, Wr[:, 0:512], tmp[:, 0:512])
                    e.tensor_scalar(out=Wi[:, 0:512], in0=Wr[:, 0:512], scalar1=math.pi / 2, scalar2=None, op0=ALU.add)
                    self.rred(e, Wi[:, 0:512], tmp[:, 0:512])
                self.V(vph)

                def asin(e):
                    e.activation(out=Stab, in_=Wr[:, 0:512], func=AF.Sin)
                    e.activation(out=Ctab, in_=Wi[:, 0:512], func=AF.Sin)
                self.A(asin)

                def vmain(e, pr=pr, k=k):
                    Vr, Vi = psA[:, 0:TMAX], psA[:, 1024:1024 + TMAX]
                    segs = [(slice(0, 512), Ctab, Stab)]
                    if has_s:
                        e.memset(Cs, 0.0); e.memset(Ss, 0.0)
                        for q in range(NSQ):
                            e.tensor_copy(out=Cs[:, 8 * q + 4:8 * q + 8], in_=Ctab[:, 0:4])
                            e.tensor_copy(out=Ss[:, 8 * q + 4:8 * q + 8], in_=Stab[:, 0:4])
                        segs.append((slice(512, TMAX), Cs, Ss))
                    for (sl, Cc, Sc) in segs:
                        TT(e, Wr[:, sl], Cc, Vr[:, sl], ALU.mult); TT(e, tmp[:, sl], Sc, Vi[:, sl], ALU.mult)
                        TT(e, Wr[:, sl], Wr[:, sl], tmp[:, sl], ALU.add)
                        TT(e, Wi[:, sl], Cc, Vi[:, sl], ALU.mult); TT(e, tmp[:, sl], Sc, Vr[:, sl], ALU.mult)
                        TT(e, Wi[:, sl], Wi[:, sl], tmp[:, sl], ALU.subtract)
                    mg = mag[:, pr:pr + 1]
                    e.tensor_tensor_scan(out=Gr[:, 0:512], data0=mg.to_broadcast([128, 512]), data1=Wr[:, 0:512],
                                         initial=Hpre[:, pr:pr + 1], op0=ALU.mult, op1=ALU.add)
                    e.tensor_tensor_scan(out=Gi[:, 0:512], data0=mg.to_broadcast([128, 512]), data1=Wi[:, 0:512],
                                         initial=Hpim[:, pr:pr + 1], op0=ALU.mult, op1=ALU.add)
                    if has_s:
                        for q in range(NSQ):
                            c0 = scol(q)
                            e.tensor_tensor_scan(out=Gr[:, c0:c0 + 4], data0=mg.to_broadcast([128, 4]), data1=Wr[:, c0:c0 + 4],
                                                 initial=h0re[:, pr, q:q + 1], op0=ALU.mult, op1=ALU.add)
                            e.tensor_tensor_scan(out=Gi[:, c0:c0 + 4], data0=mg.to_broadcast([128, 4]), data1=Wi[:, c0:c0 + 4],
                                                 initial=h0im[:, pr, q:q + 1], op0=ALU.mult, op1=ALU.add)
                    for (sl, Cc, Sc) in segs:
                        TT(e, Wr[:, sl], Cc, Gr[:, sl], ALU.mult); TT(e, tmp[:, sl], Sc, Gi[:, sl], ALU.mult)
                        TT(e, Wr[:, sl], Wr[:, sl], tmp[:, sl], ALU.subtract)
                        TT(e, Wi[:, sl], Cc, Gi[:, sl], ALU.mult); TT(e, tmp[:, sl], Sc, Gr[:, sl], ALU.mult)
                        TT(e, Wi[:, sl], Wi[:, sl], tmp[:, sl], ALU.add)
                    e.tensor_copy(out=Hpre[:, pr:pr + 1], in_=Wr[:, 511:512])
                    e.tensor_copy(out=Hpim[:, pr:pr + 1], in_=Wi[:, 511:512])
                    if has_s:
                        for q in range(NSQ):
                            c0 = scol(q) + 3
                            e.tensor_copy(out=self.s5_fre[:, q, pr:pr + 1], in_=Wr[:, c0:c0 + 1])
                            e.tensor_copy(out=self.s5_fim[:, q, pr:pr + 1], in_=Wi[:, c0:c0 + 1])
                    e.tensor_copy(out=Hbf[:, k, 0, :T], in_=Wr[:, :T])
                    e.tensor_copy(out=Hbf[:, k, 1, :T], in_=Wi[:, :T])
                self.V(vmain)

            def py(e):
                for (c0, cn) in nt:
                    n = 0
                    for k in range(4):
                        for c in range(2):
                            e.matmul(psA[:, c0:c0 + cn], lhsT=Ct[:, k, c, :], rhs=Hbf[:, k, c, c0:c0 + cn],
                                     start=(n == 0), stop=(n == 7))
                            n += 1
            self.P(py)
            self.V(lambda e, ft=ft, u=u: e.scalar_tensor_tensor(out=tmp[:, :T], in0=u[:, :T], scalar=self.s5_dA[:, ft:ft + 1],
                                                                in1=psA[:, :T], op0=ALU.mult, op1=ALU.add))
            self.A(lambda e, u=u: e.activation(out=u[:, :T], in_=tmp[:, :T], func=AF.Gelu_apprx_tanh))

    def swa_setup(self):
        d = self.din
        nc = self.nc
        self.Lx = nc.dram_tensor("Lx_d", [32, 384], F32).ap()
        self.biasD = nc.dram_tensor("bias_d", [32, 128, 256], F32).ap()
        self.ar_reset(False)
        A = self.ar_f32
        rb, oh, neg, Rsb, Ch, Bsb, Jm = A(32), A(128), A(384), A(32), A(256), A(256), A(128)
        self.dma([(rb[:32, :], d["rel_bias"]), (oh[:32, :], d["ohrev"]), (Jm, d["Jmat"]), (self.sinkb[:, :], d["sinkb"])])
        self.V(lambda e: e.memset(neg, -1e30))
        self.dma([(self.Lx, neg[:32, :])])
        self.P(lambda e: e.matmul(self.psT[:, 0:32], lhsT=oh[:32, :], rhs=rb[:32, :], start=True, stop=True))
        self.V(lambda e: e.tensor_copy(out=Rsb, in_=self.psT[:, 0:32]))
        self.dma([(self.Lx[:, 128:256].rearrange("h k -> k h"), Rsb)], allow_slow_non_contiguous=True)
        for h in range(32):
            hank = bass.AP(tensor=self.Lx.tensor, offset=h * 384, ap=[[1, 128], [1, 256]])
            self.dma([(Ch, hank)])
            self.P(lambda e: e.matmul(self.psT[:, 0:256], lhsT=Jm, rhs=Ch, start=True, stop=True))
            self.V(lambda e: e.tensor_copy(out=Bsb, in_=self.psT[:, 0:256]))
            self.dma([(self.biasD[h], Bsb)])

    def swa_pass(self, ps_idx, T, nt):
        d, actA = self.din, self.actA
        has_s = ps_idx == 0
        self.ar_reset()
        kFp = self.ar_bf(4, 128 + TMAX)
        vTp = self.ar_bf(5, 512)
        vF = self.ar_bf(4, TMAX)
        ck32, cv32 = self.ar_f32(512), self.ar_f32(512)
        f32k = self.ar_f32(8, 160)
        B = self.ar_f32(256)
        ckF = self.ar_bf(4, 128)
        cvT = self.ar_bf(512)
        knv32 = self.ar_f32(1024)
        vnT = self.ar_bf(512)

        def vh(e):
            if ps_idx == 0:
                e.memset(self.kh[:, :, :], 0.0); e.memset(self.vh[:, :], 0.0)
            e.tensor_copy(out=kFp[:, :, 0:128], in_=self.kh[:, :, :])
            e.tensor_copy(out=vTp[:, 0, :], in_=self.vh[:, :])
        self.V(vh)

        def epi_kv(e, i, j, ps):
            dst = kFp[:, i, 128:128 + TMAX] if i < 4 else vF[:, i - 4, :]
            for (c0, cn) in nt:
                e.activation(out=dst[:, c0:c0 + cn], in_=ps(c0, cn), func=AF.Copy)
            last = e.activation(out=f32k[:, i, 0:128], in_=ps(384, 128), func=AF.Copy)
            if has_s:
                last = e.activation(out=f32k[:, i, 128:160], in_=ps(512, 32), func=AF.Copy)
            return last
        self.linear(d["w_ein"], list(range(32, 40)), KT, self.xact(self.xb), nt, epi_kv, "scalar")
        for blk in range(4):
            def pt(e, blk=blk):
                for vc in range(4):
                    e.transpose(out=self.psPT[:, vc * 128:(vc + 1) * 128], in_=vF[:, vc, blk * 128:(blk + 1) * 128],
                                identity=self.identB[:, :])
            self.P(pt)
            self.V(lambda e, blk=blk: e.tensor_copy(out=vTp[:, 1 + blk, :], in_=self.psPT[:, 0:512]))
        self.linear(d["w_ein"], list(range(16, 32)), KT, self.xact(self.xb), nt,
                    self.epi_copy(actA, nt, AF.Copy, off=16), "scalar")

        def hinfo(c, s):
            j, i = c // 4, c % 4
            return j, 8 * j + i + 4 * s, 2 * j + s, slice(64 * s, 64 * s + 64)
        for c in range(16):
            for s in range(2):
                j, h, g, rows = hinfo(c, s)
                self.dma([(B, self.biasD[h])])
                for blk in range(4):
                    self.attn(128, actA[rows, 16 + c, blk * 128:(blk + 1) * 128],
                              [kFp[rows, j, blk * 128:blk * 128 + 128], kFp[rows, j, blk * 128 + 128:blk * 128 + 256]],
                              [vTp[:, blk, g * 64:(g + 1) * 64], vTp[:, blk + 1, g * 64:(g + 1) * 64]],
                              (64 * s, 64), actA[rows, 16 + c, blk * 128:(blk + 1) * 128], 0.125,
                              bias_ap=B, sink_ap=self.sinkb[:, h:h + 1],
                              extra_mask=(0, 128) if (ps_idx == 0 and blk == 0) else None)
        if has_s:
            for q in range(NSQ):
                self.dma([(ck32, d["c_swa_k"][q]), (cv32, d["c_swa_v"][q])])
                self.outdma([(self.dout["s_swa_k"][q, 0:124, :], d["c_swa_k"][q, 4:128, :]),
                             (self.dout["s_swa_v"][q, 0:124, :], d["c_swa_v"][q, 4:128, :])])

                def pk(e):
                    for vc in range(4):
                        self.tr32(e, ck32[:, vc * 128:(vc + 1) * 128], 128, 128, slot=vc)
                self.P(pk)

                def vk(e):
                    e.tensor_copy(out=ckF, in_=self.psTr[:, 0:512].rearrange("p (a b) -> p a b", a=4))
                    e.tensor_copy(out=cvT, in_=cv32)
                self.V(vk)
                cq = 128 + 8 * q + 4
                for half in range(2):
                    def pn_(e, half=half, cq=cq):
                        for vc in range(4):
                            self.tr32(e, f32k[:, half * 4 + vc, cq:cq + 4], 128, 4, slot=vc)
                    self.P(pn_)
                    self.V(lambda e, half=half: e.tensor_copy(out=knv32[:4, half * 512:(half + 1) * 512],
                                                              in_=self.psTr[:4, 0:512]))
                self.V(lambda e: e.tensor_copy(out=vnT[:4, :], in_=knv32[:4, 512:1024]))
                self.outdma([(self.dout["s_swa_k"][q, 124:128, :], knv32[:4, 0:512]),
                             (self.dout["s_swa_v"][q, 124:128, :], knv32[:4, 512:1024])])
                c0 = scol(q)
                for c in range(16):
                    for s in range(2):
                        j, h, g, rows = hinfo(c, s)
                        self.dma([(B[:4, 0:132], self.biasD[h, 0:4, 0:132])])
                        self.attn(4, actA[rows, 16 + c, c0:c0 + 4],
                                  [ckF[rows, j, :], kFp[rows, j, 128 + c0:128 + c0 + 4]],
                                  [cvT[:, g * 64:(g + 1) * 64], vnT[:4, g * 64:(g + 1) * 64]],
                                  (64 * s, 64), actA[rows, 16 + c, c0:c0 + 4], 0.125,
                                  bias_ap=B[:4, 0:132], sink_ap=self.sinkb[:4, h:h + 1])
        if ps_idx == NPASS - 1:
            for half, nm in ((0, "p_swa_k"), (1, "p_swa_v")):
                def pl(e, half=half):
                    for vc in range(4):
                        self.tr32(e, f32k[:, half * 4 + vc, 0:128], 128, 128, slot=vc)
                self.P(pl)
                self.V(lambda e: e.tensor_copy(out=ck32, in_=self.psTr[:, 0:512]))
                self.outdma([(self.dout[nm], ck32)])

        def vcarry(e):
            e.tensor_copy(out=self.kh[:, :, :], in_=kFp[:, :, 512:640])
            e.tensor_copy(out=self.vh[:, :], in_=vTp[:, 4, :])
        self.V(vcarry)

    def s5_outputs(self, ps_idx):
        self.ar_reset()
        tm = self.ar_f32(128)
        if ps_idx == 0:
            for src, nm in ((self.s5_fre, "s_s5_re"), (self.s5_fim, "s_s5_im")):
                for b in range(2):
                    self.P(lambda e, src=src, b=b: self.tr32(e, src[:, 2 * b:2 * b + 2, :].rearrange("p a b -> p (a b)"), 128, 128))
                    self.V(lambda e: e.tensor_copy(out=tm, in_=self.psTr[:, 0:128]))
                    self.outdma([(self.dout[nm][b * 128:(b + 1) * 128, :], tm)])
        if ps_idx == NPASS - 1:
            for src, nm in ((self.s5_Hre, "p_s5_re"), (self.s5_Him, "p_s5_im")):
                self.P(lambda e, src=src: self.tr32(e, src[:, :], 128, 64))
                self.V(lambda e: e.tensor_copy(out=tm[:64, :], in_=self.psTr[:64, 0:128]))
                self.outdma([(self.dout[nm], tm[:64, :])])

    def even_mixer(self, ps_idx, T, nt):
        d, actA = self.din, self.actA
        self.linear(d["w_ein"], list(range(16)), KT, self.xact(self.xb), nt,
                    self.epi_copy(actA, nt, AF.Copy, off=16), "scalar")
        self.s5_pass(ps_idx, T, nt)
        self.s5_outputs(ps_idx)
        self.linear(d["w_glu"], list(range(16)), 16, lambda kt, c0, cn: actA[:, 16 + kt, c0:c0 + cn], nt,
                    self.epi_copy(actA, nt, AF.Sigmoid, off=0), "scalar")
        self.V(lambda e: e.tensor_tensor(out=actA[:, 0:16, :T], in0=actA[:, 0:16, :T], in1=actA[:, 16:32, :T], op=ALU.mult))
        self.swa_pass(ps_idx, T, nt)
        self.linear(d["w_eout"], list(range(KT)), KT, self.xact(actA), nt, self.epi_res(True, nt), "vector")

    def hg_setup(self):
        d = self.din
        self.hst_d = self.nc.dram_tensor("hst_d", [32, 128, 128], F32).ap()
        self.ar_reset(False)
        l0, l1 = self.ar_f32(32), self.ar_f32(32)
        self.dma([(l0, d["hg_l0"]), (l1, d["hg_l1"]), (self.hg_gn[:, :], d["hg_gn"])])
        self.V(lambda e: e.tensor_tensor(out=l1, in0=l1, in1=l0, op=ALU.subtract))
        self.A(lambda e: e.activation(out=self.hg_lb[:, :], in_=l1, func=AF.Sigmoid))
        self.V(lambda e: e.tensor_scalar(out=self.hg_oml[:, :], in0=self.hg_lb[:, :], scalar1=-1.0, scalar2=1.0,
                                         op0=ALU.mult, op1=ALU.add))

    def odd_mixer(self, ps_idx, T, nt):
        d, actA, psA, psS, psT, psPT = self.din, self.actA, self.psA, self.psS, self.psT, self.psPT
        has_s = ps_idx == 0
        last_pass = ps_idx == NPASS - 1
        self.ar_reset()
        b1, b2, b3, b4 = [self.ar_f32(TMAX) for _ in range(4)]
        vF, gs, qt, kt, kl = [self.ar_bf(TMAX) for _ in range(5)]
        ecl = self.ar_f32(20)
        klT, vTt = self.ar_bf(20, 128), self.ar_bf(20, 128)
        S = self.ar_f32(128)
        Sall = self.ar_bf(16, 128)
        amT = self.ar_bf(512)
        segm = self.ar_f32(TMAX)
        tri = self.ar_f32(512)
        ss = self.st1
        self.dma([(segm, d["segmask"]), (tri[:32, :], d["trimask"])])

        def vinit(e):
            e.memset(ss, 0.0)
            if has_s:
                e.memset(actA[:, :, TP:TMAX], 0.0)
        self.V(vinit)
        TT = lambda e, o, a, b, op: e.tensor_tensor(out=o, in0=a, in1=b, op=op)
        NCH = TP // 32
        for h in range(32):
            if ps_idx == 0:
                self.V(lambda e: e.memset(S, 0.0))
            else:
                self.dma([(S, self.hst_d[h])])

            def epi(e, i, j, ps):
                last = None
                for (c0, cn) in nt:
                    if i == 0:
                        last = e.activation(out=b1[:, c0:c0 + cn], in_=ps(c0, cn), func=AF.Silu)
                    elif i == 1:
                        last = e.activation(out=b2[:, c0:c0 + cn], in_=ps(c0, cn), func=AF.Sigmoid)
                    elif i == 2:
                        last = e.activation(out=vF[:, c0:c0 + cn], in_=ps(c0, cn), func=AF.Copy)
                    else:
                        last = e.activation(out=gs[:, c0:c0 + cn], in_=ps(c0, cn), func=AF.Sigmoid)
                return last
            self.linear(d["w_oin"], [h, 32 + h, 64 + h, 96 + h], KT, self.xact(self.xb), nt, epi, "scalar")
            self.V(lambda e, h=h: e.tensor_scalar(out=b2[:, :T], in0=b2[:, :T], scalar1=self.hg_oml[:, h:h + 1],
                                                  scalar2=self.hg_lb[:, h:h + 1], op0=ALU.mult, op1=ALU.add))
            self.A(lambda e: e.activation(out=b3[:, :T], in_=b2[:, :T], func=AF.Ln))

            def v2(e):
                e.tensor_scalar(out=b2[:, :T], in0=b2[:, :T], scalar1=-1.0, scalar2=1.0, op0=ALU.mult, op1=ALU.add)
                e.tensor_tensor_scan(out=b4[:, :T], data0=segm[:, :T], data1=b3[:, :T], initial=0.0,
                                     op0=ALU.mult, op1=ALU.add)
            self.V(v2)
            def aexp(e):
                e.activation(out=b3[:, :T], in_=b4[:, :T], func=AF.Exp)
                e.activation(out=b4[:, :T], in_=b4[:, :T], func=AF.Exp, scale=-1.0)
            self.A(aexp)

            def v4(e):
                TT(e, qt[:, :T], b1[:, :T], b3[:, :T], ALU.mult)
                e.tensor_copy(out=ecl[:, 0:NCH], in_=b3[:, 31:TP:32])
                if has_s:
                    e.tensor_copy(out=ecl[:, 16:20], in_=b3[:, TP + 7:TMAX:8])
                TT(e, b1[:, :T], b2[:, :T], b4[:, :T], ALU.mult)
                e.tensor_copy(out=kt[:, :T], in_=b1[:, :T])
                for c in range(NCH):
                    e.tensor_scalar(out=kl[:, 32 * c:32 * c + 32], in0=b1[:, 32 * c:32 * c + 32],
                                    scalar1=ecl[:, c:c + 1], scalar2=None, op0=ALU.mult)
                if has_s:
                    for q in range(NSQ):
                        c0 = scol(q)
                        e.tensor_scalar(out=kl[:, c0:c0 + 4], in0=b1[:, c0:c0 + 4], scalar1=ecl[:, 16 + q:17 + q],
                                        scalar2=None, op0=ALU.mult)
            self.V(v4)
            for src, dst in ((kl, klT), (vF, vTt)):
                for r in range(2):
                    def ptr(e, src=src, r=r):
                        for cc in range(8):
                            c = 8 * r + cc
                            e.transpose(out=psPT[:32, cc * 128:(cc + 1) * 128], in_=src[:, 32 * c:32 * c + 32],
                                        identity=self.identB[:, :])
                    self.P(ptr)
                    self.V(lambda e, dst=dst, r=r: e.tensor_copy(
                        out=dst[:32, 8 * r:8 * r + 8, :], in_=psPT[:32, 0:1024].rearrange("p (a b) -> p a b", a=8)))
                if has_s:
                    def ptr2(e, src=src):
                        for q in range(NSQ):
                            c0 = scol(q)
                            e.transpose(out=psPT[:4, q * 128:(q + 1) * 128], in_=src[:, c0:c0 + 4], identity=self.identB[:, :])
                    self.P(ptr2)
                    self.V(lambda e, dst=dst: e.tensor_copy(
                        out=dst[:4, 16:20, :], in_=psPT[:4, 0:512].rearrange("p (a b) -> p a b", a=4)))

            def pat(e):
                for c in range(NCH):
                    e.matmul(psT[:32, 32 * c:32 * c + 32], lhsT=kt[:, 32 * c:32 * c + 32], rhs=qt[:, 32 * c:32 * c + 32],
                             start=True, stop=True)
                for c in range(NCH):
                    e.matmul(psA[:, 128 * c:128 * c + 128], lhsT=klT[:32, c, :], rhs=vTt[:32, c, :], start=True, stop=True)
            self.P(pat)

            def vst(e):
                TT(e, amT[:32, :], psT[:32, 0:512], tri[:32, :], ALU.mult)
                for c in range(NCH):
                    e.tensor_copy(out=Sall[:, c, :], in_=S)
                    e.scalar_tensor_tensor(out=S, in0=S, scalar=ecl[:, c:c + 1], in1=psA[:, 128 * c:128 * c + 128],
                                           op0=ALU.mult, op1=ALU.add)
            self.V(vst)

            def po(e):
                for c in range(NCH):
                    sl = slice(32 * c, 32 * c + 32)
                    e.matmul(psS[:, sl], lhsT=vTt[:32, c, :], rhs=amT[:32, sl], start=True, stop=False)
                    e.matmul(psS[:, sl], lhsT=Sall[:, c, :], rhs=qt[:, sl], start=False, stop=True)
            self.P(po)
            self.A(lambda e: e.activation(out=b3[:, 0:TP], in_=psS[:, 0:TP], func=AF.Square))

            def vo(e, h=h):
                TT(e, b4[:, 0:TP], psS[:, 0:TP], gs[:, 0:TP], ALU.mult)
                e.tensor_scalar(out=actA[:, h, 0:TP], in0=b4[:, 0:TP], scalar1=self.hg_gn[:, h:h + 1], scalar2=None, op0=ALU.mult)
                TT(e, ss[:, 0:TP], ss[:, 0:TP], b3[:, 0:TP], ALU.add)
            self.V(vo)
            if last_pass:
                self.outdma([(self.dout["p_hgrn"][h], S)])
            else:
                self.dma([(self.hst_d[h], S)])
            if has_s:
                for q in range(NSQ):
                    c0 = scol(q)
                    self.dma([(S, d["c_hg"][q, h])])

                    def p1(e, q=q, c0=c0):
                        e.matmul(psT[:4, 0:4], lhsT=kt[:, c0:c0 + 4], rhs=qt[:, c0:c0 + 4], start=True, stop=True)
                        e.matmul(psA[:, 0:128], lhsT=klT[:4, 16 + q, :], rhs=vTt[:4, 16 + q, :], start=True, stop=True)
                    self.P(p1)

                    def v1(e, q=q):
                        TT(e, amT[:4, 0:4], psT[:4, 0:4], tri[:4, 0:4], ALU.mult)
                        e.tensor_copy(out=Sall[:, 0, :], in_=S)
                        e.scalar_tensor_tensor(out=S, in0=S, scalar=ecl[:, 16 + q:17 + q], in1=psA[:, 0:128],
                                               op0=ALU.mult, op1=ALU.add)
                    self.V(v1)

                    def p2(e, q=q, c0=c0):
                        e.matmul(psS[:, 0:4], lhsT=vTt[:4, 16 + q, :], rhs=amT[:4, 0:4], start=True, stop=False)
                        e.matmul(psS[:, 0:4], lhsT=Sall[:, 0, :], rhs=qt[:, c0:c0 + 4], start=False, stop=True)
                    self.P(p2)
                    self.A(lambda e: e.activation(out=b3[:, 0:4], in_=psS[:, 0:4], func=AF.Square))

                    def v2s(e, h=h, c0=c0):
                        TT(e, b4[:, 0:4], psS[:, 0:4], gs[:, c0:c0 + 4], ALU.mult)
                        e.tensor_scalar(out=actA[:, h, c0:c0 + 4], in0=b4[:, 0:4], scalar1=self.hg_gn[:, h:h + 1],
                                        scalar2=None, op0=ALU.mult)
                        TT(e, ss[:, c0:c0 + 4], ss[:, c0:c0 + 4], b3[:, 0:4], ALU.add)
                    self.V(v2s)
                    self.outdma([(self.dout["s_hgrn"][q, h], S)])
        self.A(lambda e: e.activation(out=qt[:, :T], in_=ss[:, :T], func=AF.Copy))

        def vlo(e):
            e.tensor_copy(out=b1[:, :T], in_=qt[:, :T])
            TT(e, b1[:, :T], ss[:, :T], b1[:, :T], ALU.subtract)
            e.tensor_copy(out=kt[:, :T], in_=b1[:, :T])
        self.V(vlo)

        def pss(e):
            for (c0, cn) in nt:
                e.matmul(psS[:, c0:c0 + cn], lhsT=self.onesB[:, :], rhs=qt[:, c0:c0 + cn], start=True, stop=False)
                e.matmul(psS[:, c0:c0 + cn], lhsT=self.onesB[:, :], rhs=kt[:, c0:c0 + cn], start=False, stop=True)
        self.P(pss)
        self.V(lambda e: e.tensor_scalar(out=ss[:, :T], in0=psS[:, :T], scalar1=1.0 / D, scalar2=1e-6, op0=ALU.mult, op1=ALU.add))
        self.A(lambda e: e.activation(out=ss[:, :T], in_=ss[:, :T], func=AF.Sqrt))
        self.V(lambda e: e.reciprocal(out=ss[:, :T], in_=ss[:, :T]))
        xres = self.xres

        def epi_o(e, i, j, ps):
            last = None
            for (c0, cn) in nt:
                e.tensor_tensor(out=b2[:, c0:c0 + cn], in0=ps(c0, cn), in1=ss[:, c0:c0 + cn], op=ALU.mult)
                last = e.scalar_tensor_tensor(out=xres[:, j, c0:c0 + cn], in0=xres[:, j, c0:c0 + cn], scalar=ALPHA,
                                              in1=b2[:, c0:c0 + cn], op0=ALU.mult, op1=ALU.add)
            return last
        self.linear(d["w_oout"], list(range(KT)), KT, self.xact(actA), nt, epi_o, "vector")

    def mixer(self, l, ps_idx, T, nt):
        if DBG.get('stub'):
            def v(e):
                for j in range(KT):
                    e.tensor_scalar(out=self.xres[:, j, :T], in0=self.xres[:, j, :T], scalar1=ALPHA, scalar2=None, op0=ALU.mult)
            self.V(v)
            return
        if l == 0:
            self.even_mixer(ps_idx, T, nt)
        else:
            self.odd_mixer(ps_idx, T, nt)

    def build(self):
        nc = self.nc
        I, O = self.inp, self.outp
        I("xT", [NPASS, KT, 128, TP]); I("xsT", [KT, 128, SC]); I("memT", [KT, 128, 256])
        I("c_mem_k", [2, NSQ, 256, 512]); I("c_mem_v", [2, NSQ, 256, 512])
        I("lng", [128, 6 * KT]); I("lnb", [128, 6 * KT]); I("identF", [128, 128]); I("onesF", [128, 128])
        I("w_gate", [2, FT, 128, KT, 128]); I("w_up", [2, FT, 128, KT, 128])
        for r, (a, b) in enumerate(FF_ROUNDS):
            I("w_down%d" % r, [2, KT, 128, b - a, 128])
        for n in ("w_mq", "w_mk", "w_mv"):
            I(n, [2, 4, 128, KT, 128])
        I("w_mo", [2, KT, 128, 4, 128])
        I("w_ein", [40, 128, KT, 128]); I("w_glu", [16, 128, 16, 128]); I("w_eout", [KT, 128, KT, 128])
        I("w_oin", [128, 128, KT, 128]); I("w_oout", [KT, 128, KT, 128])
        for n in ("s5_lre", "s5_lim", "s5_ldt"):
            I(n, [128, 64])
        for n in ("s5_bre", "s5_bim", "s5_cre", "s5_cim"):
            I(n, [128, 64 * 16])
        I("s5_dA", [128, 16]); I("s5_h0re", [128, 64 * NSQ]); I("s5_h0im", [128, 64 * NSQ])
        I("rel_bias", [32, 32]); I("ohrev", [32, 128]); I("Jmat", [128, 128]); I("sinkb", [128, 32])
        I("iota1", [128, 512]); I("segmask", [128, TMAX]); I("trimask", [32, 512])
        I("hg_l0", [128, 32]); I("hg_l1", [128, 32]); I("hg_gn", [128, 32])
        I("c_swa_k", [NSQ, 128, 512]); I("c_swa_v", [NSQ, 128, 512]); I("c_hg", [NSQ, 32, 128, 128])
        O("y_p", [NPASS, KT, 128, TP]); O("y_s", [KT, 128, SC])
        O("p_mem_k", [2, 256, 512]); O("p_mem_v", [2, 256, 512])
        O("p_swa_k", [128, 512]); O("p_swa_v", [128, 512]); O("p_s5_re", [64, 128]); O("p_s5_im", [64, 128])
        O("p_hgrn", [32, 128, 128])
        O("s_swa_k", [NSQ, 128, 512]); O("s_swa_v", [NSQ, 128, 512]); O("s_s5_re", [256, 128]); O("s_s5_im", [256, 128])
        O("s_hgrn", [NSQ, 32, 128, 128])
        for nm, shp in DBG.get('extra_outputs', []):
            self.dout[nm] = self.nc.dram_tensor(nm, list(shp), F32, kind="ExternalOutput").ap()

        sb = self.sb
        self.xb = sb("xb", [128, KT, TMAX], BF16)
        self.xres = sb("xres", [128, KT, TMAX], F32)
        self.actA = sb("actA", [128, KT, TMAX], BF16)
        self.wbuf = sb("wbuf", [128, 2, KT, 128], BF16)
        self.mkF = sb("mkF", [128, 4, 256], BF16)
        self.mvT = sb("mvT", [128, 2, 512], BF16)
        self.kh = sb("kh", [128, 4, 128], BF16)
        self.vh = sb("vh", [128, 512], BF16)
        self.lng = sb("lng_sb", [128, 6 * KT], F32)
        self.lnb = sb("lnb_sb", [128, 6 * KT], F32)
        self.identF = sb("identF_sb", [128, 128], F32)
        self.identB = sb("identB_sb", [128, 128], BF16)
        self.onesB = sb("onesB_sb", [128, 128], BF16)
        self.s5_mag = sb("s5_mag", [128, 64], F32); self.s5_thr = sb("s5_thr", [128, 64], F32)
        self.s5_Hre = sb("s5_Hre", [128, 64], F32); self.s5_Him = sb("s5_Him", [128, 64], F32)
        self.s5_fre = sb("s5_fre", [128, NSQ, 64], F32); self.s5_fim = sb("s5_fim", [128, NSQ, 64], F32)
        self.s5_dA = sb("s5_dA_sb", [128, 16], F32)
        self.sinkb = sb("sinkb_sb", [128, 32], F32)
        self.hg_lb = sb("hg_lb", [128, 32], F32); self.hg_oml = sb("hg_oml", [128, 32], F32)
        self.hg_gn = sb("hg_gn_sb", [128, 32], F32)
        self.AR_WORDS = 10240
        self.AR_COMMON = 1600
        self.scr = sb("scr", [128, self.AR_WORDS], F32)
        self.ar_o = 0
        self.st1 = self.ar_f32(TMAX)
        self.at_s = self.ar_f32(512)
        self.at_mx = self.ar_f32(2)
        self.at_sm = self.ar_f32(2)
        self.at_p = self.ar_bf(512)
        self.at_pt = self.ar_bf(4, 128)
        assert self.ar_o <= self.AR_COMMON
        self.psA = nc.alloc_psum_tensor("psA", [128, 2048], F32)
        self.psS = nc.alloc_psum_tensor("psS", [128, 1024], F32)
        self.psT = nc.alloc_psum_tensor("psT", [128, 512], F32)
        self.psPT = nc.alloc_psum_tensor("psPT", [128, 1024], BF16)
        self.psTr = self.psT
        self.psO = self.psS
        self.dsem = self.sem("dsem"); self.osem = self.sem("osem")
        self.svc = self.sem("svc"); self.sac = self.sem("sac"); self.sst = self.sem("sst")
        self.sw = [self.sem("sw0"), self.sem("sw1")]
        self.smm = self.sem("smm"); self.sev = self.sem("sev")
        if DBG.get('program') is not None:
            DBG['program'](self)
        else:
            self.program()
        self.flush()
        return nc

    def program(self):
        nc = self.nc
        d = self.din
        self.dma([(self.lng[:, :], d["lng"]), (self.lnb[:, :], d["lnb"]), (self.identF[:, :], d["identF"])])
        self.dma([(self.identB[:, :], d["identF"]), (self.onesB[:, :], d["onesF"])], eng="gpsimd")
        if not DBG.get('stub'):
            self.s5_setup()
            self.swa_setup()
            self.hg_setup()

        for ps_idx in range(DBG.get('npass', NPASS)):
            T = TMAX if ps_idx == 0 else TP
            nt = [(0, TP)] + ([(TP, SC)] if ps_idx == 0 else [])
            loads = [(self.xres[:, :, 0:TP], d["xT"][ps_idx].rearrange("j p t -> p j t"))]
            if ps_idx == 0:
                loads.append((self.xres[:, :, TP:TMAX], d["xsT"].rearrange("j p t -> p j t")))
            self.dma(loads)
            self.A(lambda e, T=T: e.activation(out=self.xb[:, :, :T], in_=self.xres[:, :, :T], func=AF.Copy))
            for l in range(DBG.get('nlayers', 2)):
                self.mixer(l, ps_idx, T, nt)
                self.layernorm(T, nt, l * 3 + 0)
                self.carve_cross()
                self.mem_kv(l, ps_idx == 0)
                self.cross(T, nt, l, ps_idx)
                self.layernorm(T, nt, l * 3 + 1)
                self.ffn(T, nt, l)
                self.layernorm(T, nt, l * 3 + 2)
            outs = [(self.dout["y_p"][ps_idx].rearrange("j p t -> p j t"), self.xres[:, :, 0:TP])]
            if ps_idx == 0:
                outs.append((self.dout["y_s"].rearrange("j p t -> p j t"), self.xres[:, :, TP:TMAX]))
            self.outdma(outs)


def _tile_w(w):
    K, N = w.shape
    return np.ascontiguousarray(w.reshape(K // 128, 128, N // 128, 128).transpose(2, 1, 0, 3))


def _fm(x):
    T, F = x.shape
    return np.ascontiguousarray(x.T.reshape(F // 128, 128, T))


def _t5_bucket(dist):
    n = np.maximum(dist, 0)
    nf = np.maximum(n, 1).astype(np.float32)
    large = 16 + (np.log(nf / np.float32(16)) / np.float32(math.log(128 / 16)) * np.float32(16)).astype(np.int32)
    large = np.minimum(large, 31)
    return np.where(n < 16, n, large)


def _qperm():
    cols = []
    for c in range(16):
        j, i = c // 4, c % 4
        for h in (8 * j + i, 8 * j + 4 + i):
            cols.extend(range(h * 64, h * 64 + 64))
    return np.array(cols)


def _pl(a):
    s = a.shape
    a = a.reshape((64, 2, 64) + s[2:])
    perm = (1, 2, 0) + tuple(range(3, a.ndim))
    return np.ascontiguousarray(a.transpose(perm)).reshape((128, 64) + s[2:])


_CACHE = {}
PCORES = [0, 1, 2, 3]


def kernel(**inp):
    f32 = np.float32
    g = {k: np.asarray(v) for k, v in inp.items()}
    sh = {}
    sh["w_gate"] = np.stack([_tile_w(g["w_ffn_gate"][l]) for l in range(2)])
    sh["w_up"] = np.stack([_tile_w(g["w_ffn_up"][l]) for l in range(2)])
    for r, (a, b) in enumerate(FF_ROUNDS):
        sh["w_down%d" % r] = np.stack([_tile_w(g["w_ffn_down"][l][a * 128:b * 128]) for l in range(2)])
    for n, s in (("w_mq", "w_mem_q"), ("w_mk", "w_mem_k"), ("w_mv", "w_mem_v"), ("w_mo", "w_mem_o")):
        sh[n] = np.stack([_tile_w(g[s][l]) for l in range(2)])
    qp = _qperm()
    win = g["w_even_in"][0]
    cols = np.concatenate([np.arange(2048), 2048 + qp, np.arange(4096, 5120)])
    sh["w_ein"] = _tile_w(win[:, cols])
    sh["w_glu"] = _tile_w(g["s5_w_glu"][0])
    rows = np.concatenate([np.arange(2048), 2048 + qp])
    sh["w_eout"] = _tile_w(g["w_even_out"][0][rows])
    sh["w_oin"] = _tile_w(g["w_odd_in"][0])
    sh["w_oout"] = _tile_w(g["w_odd_out"][0])
    sh["lng"] = np.ascontiguousarray(g["ln_g"].reshape(6, KT, 128).transpose(2, 0, 1).reshape(128, 6 * KT))
    sh["lnb"] = np.ascontiguousarray(g["ln_b"].reshape(6, KT, 128).transpose(2, 0, 1).reshape(128, 6 * KT))
    sh["identF"] = np.eye(128, dtype=f32)
    sh["onesF"] = np.ones((128, 128), f32)
    sh["s5_lre"] = _pl(g["s5_lam_re"][0]); sh["s5_lim"] = _pl(g["s5_lam_im"][0])
    sh["s5_ldt"] = _pl(np.broadcast_to(g["s5_log_dt"][0][:, None], (128, 64)).copy())
    sh["s5_bre"] = _pl(g["s5_b_re"][0]).reshape(128, 1024); sh["s5_bim"] = _pl(g["s5_b_im"][0]).reshape(128, 1024)
    sh["s5_cre"] = _pl(np.ascontiguousarray(g["s5_c_re"][0].transpose(0, 2, 1))).reshape(128, 1024)
    sh["s5_cim"] = _pl(np.ascontiguousarray(g["s5_c_im"][0].transpose(0, 2, 1))).reshape(128, 1024)
    sh["s5_dA"] = np.ascontiguousarray(g["s5_d"][0].reshape(16, 128).T)
    sh["rel_bias"] = np.ascontiguousarray(g["rel_bias"])
    bk = _t5_bucket(127 - np.arange(128))
    oh = np.zeros((32, 128), f32); oh[bk, np.arange(128)] = 1.0
    sh["ohrev"] = oh
    sh["Jmat"] = np.ascontiguousarray(np.eye(128, dtype=f32)[::-1])
    sh["sinkb"] = np.ascontiguousarray(np.broadcast_to(g["swa_sinks"][0][None, :], (128, 32))).astype(f32)
    sh["iota1"] = np.ascontiguousarray(np.broadcast_to(np.arange(1, 513, dtype=f32)[None, :], (128, 512)))
    seg = np.ones(TMAX, f32); seg[0:TP:32] = 0.0; seg[TP:] = 0.0
    for q in range(NSQ):
        seg[TP + 8 * q + 5:TP + 8 * q + 8] = 1.0
    sh["segmask"] = np.ascontiguousarray(np.broadcast_to(seg[None, :], (128, TMAX)))
    tri = np.triu(np.ones((32, 32), f32))
    sh["trimask"] = np.ascontiguousarray(np.tile(tri, (1, 16)))
    sh["hg_l0"] = np.ascontiguousarray(g["hg_lb_logits"][0].reshape(32, 128).T)
    sh["hg_l1"] = np.ascontiguousarray(g["hg_lb_logits"][1].reshape(32, 128).T)
    sh["hg_gn"] = np.ascontiguousarray(g["hg_norm_g"][0].reshape(32, 128).T)
    in_maps = []
    for c in range(8):
        m = dict(sh)
        if c in PCORES:
            b = PCORES.index(c)
            m["xT"] = np.stack([_fm(g["x_prompt"][b, p * TP:(p + 1) * TP]) for p in range(NPASS)])
            m["memT"] = _fm(g["mem_prompt"][b])
        else:
            m["xT"] = np.zeros((NPASS, KT, 128, TP), f32)
            m["memT"] = np.zeros((KT, 128, 256), f32)
        xs = np.zeros((SC, D), f32)
        for q in range(NSQ):
            xs[8 * q + 4:8 * q + 8] = g["x_sample"][c * NSQ + q]
        m["xsT"] = _fm(xs)
        sl = slice(c * NSQ, (c + 1) * NSQ)
        m["c_mem_k"] = np.ascontiguousarray(g["cache_mem_k"][:, sl].reshape(2, NSQ, 256, 512))
        m["c_mem_v"] = np.ascontiguousarray(g["cache_mem_v"][:, sl].reshape(2, NSQ, 256, 512))
        m["c_swa_k"] = np.ascontiguousarray(g["cache_swa_k"][0, sl].reshape(NSQ, 128, 512))
        m["c_swa_v"] = np.ascontiguousarray(g["cache_swa_v"][0, sl].reshape(NSQ, 128, 512))
        m["c_hg"] = np.ascontiguousarray(g["state_hgrn"][0, sl])
        for nm, src in (("s5_h0re", "state_s5_re"), ("s5_h0im", "state_s5_im")):
            a = g[src][0, sl]
            a = a.reshape(NSQ, 64, 2, 64).transpose(2, 3, 1, 0)
            m[nm] = np.ascontiguousarray(a).reshape(128, 64 * NSQ)
        in_maps.append(m)
    if "nc" not in _CACHE:
        _CACHE["nc"] = Builder().build()
    res = run_bass_kernel_spmd(_CACHE["nc"], in_maps, core_ids=list(range(8))).results
    y_p = np.stack([np.concatenate([res[c]["y_p"][p].reshape(D, TP).T for p in range(NPASS)], 0) for c in PCORES])
    y_s = np.zeros((32, 4, D), f32)
    for c in range(8):
        ys = res[c]["y_s"].reshape(D, SC).T
        for q in range(NSQ):
            y_s[c * NSQ + q] = ys[8 * q + 4:8 * q + 8]
    P4 = lambda nm: np.stack([res[c][nm] for c in PCORES])
    S8 = lambda nm: np.concatenate([res[c][nm] for c in range(8)], 0)
    p_mem_k = P4("p_mem_k").transpose(1, 0, 2, 3).reshape(2, 4, 256, 4, 128)
    p_mem_v = P4("p_mem_v").transpose(1, 0, 2, 3).reshape(2, 4, 256, 4, 128)
    p_swa_k = P4("p_swa_k").reshape(1, 4, 128, 8, 64)
    p_swa_v = P4("p_swa_v").reshape(1, 4, 128, 8, 64)
    p_s5_re = P4("p_s5_re").reshape(1, 4, 128, 64)
    p_s5_im = P4("p_s5_im").reshape(1, 4, 128, 64)
    p_hg = P4("p_hgrn").reshape(1, 4, 32, 128, 128)
    s_swa_k = S8("s_swa_k").reshape(1, 32, 128, 8, 64)
    s_swa_v = S8("s_swa_v").reshape(1, 32, 128, 8, 64)
    s_s5_re = S8("s_s5_re").reshape(1, 32, 128, 64)
    s_s5_im = S8("s_s5_im").reshape(1, 32, 128, 64)
    s_hg = S8("s_hgrn").reshape(1, 32, 32, 128, 128)
    outs = (y_p, y_s, p_mem_k, p_mem_v, p_swa_k, p_swa_v, p_s5_re, p_s5_im, p_hg,
            s_swa_k, s_swa_v, s_s5_re, s_s5_im, s_hg)
    return tuple(np.ascontiguousarray(o, dtype=f32) for o in outs)
```
